# Optimizing a Trainium2 kernel written in Bass

```python
import jax
import jax.numpy as jnp
from jax import lax
import numpy as np

D_MODEL = 2048
BATCH = 2
SEQ = 4096
DEPTH = 1

RET_HEADS = 4
RET_DV = D_MODEL // RET_HEADS
RET_DK = RET_DV // 2
RET_CHUNK = 256
RET_THETA = 10000.0
ATT_HEAD_DIM = 64
ATT_HEADS = D_MODEL // ATT_HEAD_DIM
ATT_KV_HEADS = ATT_HEADS // 8
WINDOW = 128
ATT_BLOCK = WINDOW
ROPE_THETA = 500000.0
ROPE_DIM = ATT_HEAD_DIM // 4
D_FF = ((-(-8 * D_MODEL // 3) + 255) // 256) * 256
DEEPNORM_ALPHA = (2.0 * DEPTH) ** 0.25
DEEPNORM_BETA = (8.0 * DEPTH) ** -0.25
LN_EPS = 1e-5
GN_EPS = 1e-5

kernel_name = 'hybrid_retention_swa_sink_deepnorm'


def _in_proj_sizes():
    return (RET_HEADS * RET_DK, RET_HEADS * RET_DK, RET_HEADS * RET_DV, RET_HEADS * RET_DV,
            ATT_HEADS * ATT_HEAD_DIM, ATT_KV_HEADS * ATT_HEAD_DIM, ATT_KV_HEADS * ATT_HEAD_DIM,
            D_MODEL, D_MODEL)


def _in_proj_scales():
    b = DEEPNORM_BETA
    return (1.0, 1.0, b, 1.0, 1.0, 1.0, b, 1.0, 1.0)


def layer_norm(x, g, b):
    xf = x.astype(jnp.float32)
    mu = jnp.mean(xf, axis=-1, keepdims=True)
    var = jnp.mean(jnp.square(xf - mu), axis=-1, keepdims=True)
    return ((xf - mu) * lax.rsqrt(var + LN_EPS)).astype(x.dtype) * g + b


def rotary_tables(positions, inv_freq, dtype):
    ang = positions.astype(jnp.float32)[..., None] * inv_freq
    return (jnp.cos(ang)[:, :, None, :].astype(dtype),
            jnp.sin(ang)[:, :, None, :].astype(dtype))


def apply_rotary(x, cos, sin):
    x1, x2 = jnp.split(x, 2, axis=-1)
    return jnp.concatenate([x1 * cos - x2 * sin, x2 * cos + x1 * sin], axis=-1)


def retention_branch(q, k, v, g, gn_g, gn_b, positions):
    bsz, seq, nh, dk = q.shape
    dv = v.shape[-1]
    inv_freq = 1.0 / (RET_THETA ** jnp.linspace(0.0, 1.0, dk // 2, dtype=jnp.float32))
    cos, sin = rotary_tables(positions, inv_freq, q.dtype)
    q = apply_rotary(q, cos, sin) * (dk ** -0.5)
    k = apply_rotary(k, cos, sin)
    log_gamma = jnp.log(1.0 - jnp.exp2(-5.0 - jnp.arange(nh, dtype=jnp.float32)))
    c = RET_CHUNK
    n = seq // c
    qc = q.reshape(bsz, n, c, nh, dk)
    kc = k.reshape(bsz, n, c, nh, dk)
    vc = v.reshape(bsz, n, c, nh, dv)
    idx = jnp.arange(c, dtype=jnp.float32)
    diff = idx[:, None] - idx[None, :]
    intra_decay = jnp.where(diff >= 0.0,
                            jnp.exp(log_gamma[:, None, None] * jnp.maximum(diff, 0.0)), 0.0)
    s = jnp.einsum('bnchd,bnmhd->bnhcm', qc, kc) * intra_decay.astype(q.dtype)
    inner = jnp.einsum('bnhcm,bnmhe->bnche', s, vc)
    k_decay = jnp.exp(log_gamma[:, None] * (c - 1.0 - idx)[None, :])
    q_decay = jnp.exp(log_gamma[:, None] * (idx + 1.0)[None, :])
    chunk_decay = jnp.exp(log_gamma * c)[None, :, None, None]
    kv = jnp.einsum('bnmhd,hm,bnmhe->nbhde', kc.astype(jnp.float32), k_decay,
                    vc.astype(jnp.float32))

    def step(state, kv_chunk):
        return state * chunk_decay + kv_chunk, state

    _, prev = lax.scan(step, jnp.zeros(kv.shape[1:], jnp.float32), kv)
    cross = jnp.einsum('bnchd,hc,nbhde->bnche', qc.astype(jnp.float32), q_decay, prev)
    y = (inner.astype(jnp.float32) + cross).reshape(bsz, seq, nh, dv)
    mu = jnp.mean(y, axis=-1, keepdims=True)
    var = jnp.mean(jnp.square(y - mu), axis=-1, keepdims=True)
    y = ((y - mu) * lax.rsqrt(var + GN_EPS)).reshape(bsz, seq, nh * dv).astype(v.dtype)
    y = y * gn_g + gn_b
    return jax.nn.silu(g) * y


def sliding_window_sink_attention(q, k, v, sinks, positions):
    bsz, seq, hq, d = q.shape
    hkv = k.shape[2]
    grp = hq // hkv
    inv_freq = 1.0 / (ROPE_THETA ** (jnp.arange(0, ROPE_DIM, 2, dtype=jnp.float32) / ROPE_DIM))
    cos, sin = rotary_tables(positions, inv_freq, q.dtype)
    q = jnp.concatenate([apply_rotary(q[..., :ROPE_DIM], cos, sin), q[..., ROPE_DIM:]], axis=-1)
    k = jnp.concatenate([apply_rotary(k[..., :ROPE_DIM], cos, sin), k[..., ROPE_DIM:]], axis=-1)
    bq = ATT_BLOCK
    n = seq // bq
    qb = q.reshape(bsz, n, bq, hkv, grp, d)
    kb = k.reshape(bsz, n, bq, hkv, d)
    vb = v.reshape(bsz, n, bq, hkv, d)
    kpad = jnp.zeros_like(kb[:, :1])
    vpad = jnp.zeros_like(vb[:, :1])
    kw = jnp.concatenate([jnp.concatenate([kpad, kb[:, :-1]], axis=1), kb], axis=2)
    vw = jnp.concatenate([jnp.concatenate([vpad, vb[:, :-1]], axis=1), vb], axis=2)
    s = jnp.einsum('bnqhgd,bnkhd->bnhgqk', qb, kw).astype(jnp.float32) * (d ** -0.5)
    qi = jnp.arange(bq)[:, None]
    kj = jnp.arange(2 * bq)[None, :]
    dist = qi + bq - kj
    band = (dist >= 0) & (dist < WINDOW)
    valid = (jnp.arange(n)[:, None, None] > 0) | (kj[None] >= bq)
    mask = band[None] & valid
    s = jnp.where(mask[None, :, None, None], s, -jnp.inf)
    sink = sinks.astype(jnp.float32).reshape(hkv, grp)[:, :, None, None]
    m = jnp.maximum(jnp.max(s, axis=-1, keepdims=True), sink)
    p = jnp.exp(s - m)
    p = p / (jnp.sum(p, axis=-1, keepdims=True) + jnp.exp(sink - m))
    o = jnp.einsum('bnhgqk,bnkhd->bnqhgd', p.astype(v.dtype), vw)
    return o.reshape(bsz, seq, hq * d)


def setup_inputs(seed: int = 0) -> dict:
    key = jax.random.key(seed)
    ks = jax.random.split(key, 16)
    f32 = jnp.float32
    x = jax.random.normal(ks[0], (BATCH, SEQ, D_MODEL), f32)
    start = jax.random.randint(ks[1], (BATCH, 1), 0, 1024, dtype=jnp.int32)
    positions = start + jnp.arange(SEQ, dtype=jnp.int32)[None, :]
    sizes = _in_proj_sizes()
    scales = _in_proj_scales()
    seg_keys = jax.random.split(ks[2], len(sizes))
    w_in = jnp.concatenate(
        [jax.random.normal(sk, (DEPTH, D_MODEL, sz), f32) * (D_MODEL ** -0.5 * sc)
         for sk, sz, sc in zip(seg_keys, sizes, scales)], axis=-1)
    ret_gn_g = 1.0 + 0.02 * jax.random.normal(ks[3], (DEPTH, RET_HEADS * RET_DV), f32)
    ret_gn_b = 0.02 * jax.random.normal(ks[4], (DEPTH, RET_HEADS * RET_DV), f32)
    att_sinks = 0.5 * jax.random.normal(ks[5], (DEPTH, ATT_HEADS), f32)
    w_out = jax.random.normal(ks[6], (DEPTH, D_MODEL, D_MODEL), f32) * (D_MODEL ** -0.5 * DEEPNORM_BETA)
    ln1_g = 1.0 + 0.02 * jax.random.normal(ks[7], (DEPTH, D_MODEL), f32)
    ln1_b = 0.02 * jax.random.normal(ks[8], (DEPTH, D_MODEL), f32)
    w_gate = jax.random.normal(ks[9], (DEPTH, D_MODEL, D_FF), f32) * (D_MODEL ** -0.5 * DEEPNORM_BETA)
    w_up = jax.random.normal(ks[10], (DEPTH, D_MODEL, D_FF), f32) * (D_MODEL ** -0.5 * DEEPNORM_BETA)
    w_down = jax.random.normal(ks[11], (DEPTH, D_FF, D_MODEL), f32) * (D_FF ** -0.5 * DEEPNORM_BETA)
    ln2_g = 1.0 + 0.02 * jax.random.normal(ks[12], (DEPTH, D_MODEL), f32)
    ln2_b = 0.02 * jax.random.normal(ks[13], (DEPTH, D_MODEL), f32)
    return {'x': x, 'positions': positions, 'w_in': w_in, 'ret_gn_g': ret_gn_g,
            'ret_gn_b': ret_gn_b, 'att_sinks': att_sinks, 'w_out': w_out,
            'ln1_g': ln1_g, 'ln1_b': ln1_b, 'w_gate': w_gate, 'w_up': w_up,
            'w_down': w_down, 'ln2_g': ln2_g, 'ln2_b': ln2_b}


def reference(x, positions, w_in, ret_gn_g, ret_gn_b, att_sinks, w_out, ln1_g, ln1_b,
              w_gate, w_up, w_down, ln2_g, ln2_b):
    bsz, seq, _ = x.shape
    splits = np.cumsum(_in_proj_sizes())[:-1].tolist()
    h = x
    for l in range(DEPTH):
        proj = h @ w_in[l]
        rq, rk, rv, rg, aq, ak, av, gate_a, gate_b = jnp.split(proj, splits, axis=-1)
        ret = retention_branch(rq.reshape(bsz, seq, RET_HEADS, RET_DK),
                               rk.reshape(bsz, seq, RET_HEADS, RET_DK),
                               rv.reshape(bsz, seq, RET_HEADS, RET_DV),
                               rg, ret_gn_g[l], ret_gn_b[l], positions)
        att = sliding_window_sink_attention(aq.reshape(bsz, seq, ATT_HEADS, ATT_HEAD_DIM),
                                            ak.reshape(bsz, seq, ATT_KV_HEADS, ATT_HEAD_DIM),
                                            av.reshape(bsz, seq, ATT_KV_HEADS, ATT_HEAD_DIM),
                                            att_sinks[l], positions)
        merged = jax.nn.sigmoid(gate_a) * ret + jax.nn.sigmoid(gate_b) * att
        h = layer_norm(DEEPNORM_ALPHA * h + merged @ w_out[l], ln1_g[l], ln1_b[l])
        ffn = (jax.nn.silu(h @ w_gate[l]) * (h @ w_up[l])) @ w_down[l]
        h = layer_norm(DEEPNORM_ALPHA * h + ffn, ln2_g[l], ln2_b[l])
    return h
```

```python
import contextlib
import numpy as np
import ml_dtypes
import concourse.bass as bass
import concourse.mybir as mybir
from concourse.bass_utils import run_bass_kernel_spmd

F32 = mybir.dt.float32; BF16 = mybir.dt.bfloat16; I32 = mybir.dt.int32
AF = mybir.ActivationFunctionType; ALU = mybir.AluOpType

D = 2048; SEQ = 4096; T = 1024; NPRI = 3072; NALL = 4096
DFF = 5632
RH = 4; RDK = 256; RDV = 512; CH = 256
ALPHA = 2.0 ** 0.25
PI = float(np.pi); TWO_PI = float(2 * np.pi); C1 = 6.28125; C2 = float(2 * np.pi - 6.28125)
SB_BASE = 16640; SB_LIMIT = 229376
NEG = -30000.0
STRICT_SAME_ENGINE = False

class Buf:
    __slots__ = ("name", "writers", "readers", "sem", "semcnt", "lo", "hi")
    def __init__(self, name, lo=None, hi=None):
        self.name = name; self.writers = {}; self.readers = {}; self.sem = None; self.semcnt = 0
        self.lo = lo; self.hi = hi

class Op:
    __slots__ = ("eng", "fn", "deps", "sig", "cnt", "idx", "dma", "sembuf", "dval", "gi")
    def __init__(self, eng, fn, dma, gi):
        self.eng = eng; self.fn = fn; self.deps = []; self.sig = False; self.cnt = None
        self.idx = None; self.dma = dma; self.sembuf = None; self.dval = None; self.gi = gi

ENGS = ("pe", "act", "dve", "pool", "sp")

class Prog:
    def __init__(self):
        self.ops = {e: [] for e in ENGS}
        self.nops = 0
        self.live = []
    def region(self, name, lo, hi):
        nb = Buf(name, lo, hi)
        keep = []
        for b in self.live:
            if b.lo < hi and lo < b.hi:
                for dct_new, dct_old in ((nb.writers, b.writers), (nb.readers, b.readers)):
                    for k, o in dct_old.items():
                        if k not in dct_new or dct_new[k].gi < o.gi:
                            dct_new[k] = o
                for rlo, rhi in ((b.lo, lo), (hi, b.hi)):
                    if rlo < rhi:
                        rb = Buf(b.name + "_rem", rlo, rhi)
                        rb.writers = dict(b.writers); rb.readers = dict(b.readers)
                        keep.append(rb)
            else:
                keep.append(b)
        keep.append(nb)
        self.live = keep
        return nb
    def add(self, eng, fn, reads=(), writes=(), dma=False, sembuf=None):
        op = Op(eng, fn, dma, self.nops)
        key = ("dma", self.nops) if dma else eng
        self.nops += 1
        deps = {}; raw = set()
        for b in reads:
            for d in b.writers.values():
                deps[id(d)] = d; raw.add(id(d))
        for b in writes:
            for d in b.writers.values(): deps[id(d)] = d
            for d in b.readers.values(): deps[id(d)] = d
        for i, d in deps.items():
            if (not d.dma) and d.eng == eng:
                if eng == "pe" or (not STRICT_SAME_ENGINE and i not in raw):
                    continue
            op.deps.append(d)
            if not d.dma: d.sig = True
        for b in reads: b.readers[key] = op
        for b in writes:
            b.writers = {key: op}; b.readers = {}
        if dma:
            sb = sembuf if sembuf is not None else (writes[0] if writes else reads[0])
            op.sembuf = sb; sb.semcnt += 16; op.dval = sb.semcnt
        op.idx = len(self.ops[eng]); self.ops[eng].append(op)
        return op
    def emit(self, block, sems):
        esem = {e: sems() for e in ENGS}
        for e in ENGS:
            c = 0
            for op in self.ops[e]:
                if op.dma:
                    if op.sembuf.sem is None: op.sembuf.sem = sems()
                elif op.sig:
                    c += 1; op.cnt = c
        def body(e_name):
            def _f(e):
                waited = {}; wdma = {}
                for op in self.ops[e_name]:
                    for d in op.deps:
                        if d.dma:
                            sb = d.sembuf
                            if wdma.get(id(sb), 0) < d.dval:
                                e.wait_ge(sb.sem, d.dval); wdma[id(sb)] = d.dval
                        else:
                            if waited.get(d.eng, -1) < d.idx:
                                e.wait_ge(esem[d.eng], d.cnt); waited[d.eng] = d.idx
                    ins = op.fn(e)
                    if ins is None: continue
                    if op.dma: ins.then_inc(op.sembuf.sem, 16)
                    elif op.sig: ins.then_inc(esem[e_name], 1)
            return _f
        block.tensor(body("pe")); block.scalar(body("act")); block.vector(body("dve"))
        block.gpsimd(body("pool")); block.sync(body("sp"))

NCF = 1 + 1 + 8 + 96 + 2048 + 1024 + 4
CF_INVR = 0; CF_INVA = 1; CF_KDEC = 2; CF_WPRI = 10; CF_DM = 106; CF_QDEC = 106 + 2048; CF_MISC = 106 + 2048 + 1024
NCB = 128 * 4 + 512 * 3
CB_ID = 0; CB_RM = 128; CB_ONES = 256; CB_MCUR = 384; CB_MPREV = 896; CB_MPREV0 = 1408; CB_SW = 1920

def _const_tables(j):
    cf = np.zeros((128, NCF), np.float32)
    p = np.arange(128)
    cf[:, CF_INVR] = (1.0 / (np.float32(10000.0) ** np.linspace(0.0, 1.0, 128, dtype=np.float32))).astype(np.float32)
    inva = (1.0 / (np.float32(500000.0) ** (np.arange(0, 16, 2, dtype=np.float32) / np.float32(16)))).astype(np.float32)
    pm = p % 64
    cf[:, CF_INVA] = np.where(pm < 16, inva[pm % 8], 0.0)
    gam = 1.0 - np.exp2(-5.0 - np.arange(4, dtype=np.float64))
    lg = np.log(gam)
    for h in range(4):
        for mt in range(2):
            m = mt * 128 + p
            cf[:, CF_KDEC + h * 2 + mt] = np.exp(lg[h] * (255.0 - m))
            c = np.arange(256)
            diff = c[None, :] - m[:, None]
            cf[:, CF_DM + (h * 2 + mt) * 256: CF_DM + (h * 2 + mt + 1) * 256] = \
                np.where(diff >= 0, np.exp(lg[h] * np.maximum(diff, 0)), 0.0) * (RDK ** -0.5)
        for tl in range(24):
            m = tl * 128 + p
            valid = m >= (NPRI - 1024 * j)
            cf[:, CF_WPRI + h * 24 + tl] = np.where(valid, np.exp(lg[h] * (NPRI - 1.0 - m)), 0.0)
        cf[:, CF_QDEC + h * 256: CF_QDEC + (h + 1) * 256] = (np.exp(lg[h] * (np.arange(256) + 1.0)) * (RDK ** -0.5))[None, :]
    cf[:, CF_MISC] = 1e-5
    cb = np.zeros((128, NCB), np.float32)
    cb[:, CB_ID:CB_ID + 128] = np.eye(128)
    rm = np.zeros((128, 128), np.float32)
    for m in range(128):
        mm = m % 64
        if mm < 8: rm[m + 8, m] = -1.0
        elif mm < 16: rm[m - 8, m] = 1.0
    cb[:, CB_RM:CB_RM + 128] = rm
    cb[:, CB_ONES:CB_ONES + 128] = 1.0
    for m in range(128):
        cb[(m + 64) % 128, CB_SW + m] = 1.0
    kk = p[:, None]; qq = np.arange(128)[None, :]
    mcur = np.where(kk <= qq, 0.0, NEG); mprev = np.where(kk > qq, 0.0, NEG)
    mprev0 = mprev if j > 0 else np.full((128, 128), NEG)
    cb[:, CB_MCUR:CB_MCUR + 512] = np.tile(mcur, (1, 4))
    cb[:, CB_MPREV:CB_MPREV + 512] = np.tile(mprev, (1, 4))
    cb[:, CB_MPREV0:CB_MPREV0 + 512] = np.tile(mprev0, (1, 4))
    return cf, cb.astype(ml_dtypes.bfloat16)

GAM_CD = [float((1.0 - 2.0 ** (-5 - h)) ** 256) for h in range(4)]

class _Cut(Exception):
    pass

def build(stage="full", cut=0):
    nc = bass.Bass("TRN2", target_bir_lowering=False)
    def din(name, shape, dt=F32):
        return nc.dram_tensor(name, shape, dt, kind="ExternalInput").ap()
    xall = din("xall", [NALL, D]); pos = din("pos", [1, NALL], I32)
    w_in = din("w_in", [D, 12800]); w_out = din("w_out", [D, D])
    w_gate = din("w_gate", [D, DFF]); w_up = din("w_up", [D, DFF]); w_down = din("w_down", [DFF, D])
    vecs = {n: din(n, [1, D]) for n in ("gn_g", "gn_b", "ln1_g", "ln1_b", "ln2_g", "ln2_b")}
    sinks = din("sinks", [1, 32]); cf_d = din("cf", [128, NCF]); cb_d = din("cb", [128, NCB], BF16)
    out_d = nc.dram_tensor("out", [T, D], F32, kind="ExternalOutput").ap()
    dbg_d = nc.dram_tensor("dbg", [T, D], F32, kind="ExternalOutput").ap() if stage != "full" else None

    P = Prog()
    names = [0]
    def sbt(shape, dt, off):
        names[0] += 1
        nbytes = int(np.prod(shape[1:])) * (2 if dt == BF16 else 4)
        assert off % 32 == 0 and off >= SB_BASE and off + nbytes <= SB_LIMIT, (shape, off, nbytes)
        return nc.alloc_sbuf_tensor_at(f"t{names[0]}", list(shape), dt, offset=off)
    class Lay:
        def __init__(self, start): self.o = start
        def take(self, shape, dt, name=None, track=True):
            nbytes = int(np.prod(shape[1:])) * (2 if dt == BF16 else 4)
            nbytes = (nbytes + 31) // 32 * 32
            t = sbt(shape, dt, self.o)
            b = P.region(name or f"r{self.o}", self.o, self.o + nbytes) if track else None
            self.o += nbytes
            return t, b

    wv_in = w_in.rearrange("(c p) n -> p c n", p=128)
    def psum_alloc(es):
        banks = []
        for i in range(8):
            t = es.enter_context(nc.psum_tensor(f"ps{i}", [128, 512], F32))
            banks.append((t, Buf(f"ps{i}")))
        return banks

    with contextlib.ExitStack() as es:
        banks = psum_alloc(es)
        bctr = [0]
        def bank():
            b = banks[bctr[0] % 8]; bctr[0] += 1; return b

        L0 = Lay(SB_BASE)
        cf, cfB = L0.take([128, NCF], F32, "cf")
        cb, cbB = L0.take([128, NCB], BF16, "cb")
        es_all, esB = L0.take([128, 32], F32, "es")
        P.add("sp", lambda e: e.dma_start(out=cf[:], in_=cf_d), writes=[cfB], dma=True)
        P.add("sp", lambda e: e.dma_start(out=cb[:], in_=cb_d), writes=[cbB], dma=True)
        P.add("sp", lambda e: e.dma_start(out=es_all[:], in_=sinks.partition_broadcast(128)), writes=[esB], dma=True)
        P.add("act", lambda e: e.activation(out=es_all[:], in_=es_all[:], func=AF.Exp), reads=[esB], writes=[esB])
        ident = cb[:, CB_ID:CB_ID + 128]; Rm = cb[:, CB_RM:CB_RM + 128]; ones = cb[:, CB_ONES:CB_ONES + 128]; SWm = cb[:, CB_SW:CB_SW + 128]
        eps_col = cf[:, CF_MISC:CF_MISC + 1]
        PERS_END = L0.o
        SIN_OFF = PERS_END

        def mm(out, lhsT, rhs, start, stop, reads, wb):
            P.add("pe", lambda e: e.matmul(out, lhsT=lhsT, rhs=rhs, start=start, stop=stop), reads=reads, writes=[wb])
        def tr(out, in_, reads, wb):
            P.add("pe", lambda e: e.transpose(out=out, in_=in_, identity=ident), reads=list(reads) + [cbB], writes=[wb])
        def act(out, in_, func, reads, writes, **kw):
            P.add("act", lambda e: e.activation(out=out, in_=in_, func=func, **kw), reads=reads, writes=writes)
        def tt(eng, out, in0, in1, op, reads, writes):
            P.add(eng, lambda e: e.tensor_tensor(out=out, in0=in0, in1=in1, op=op), reads=reads, writes=writes)
        def ts(eng, out, in0, s1, s2, op0, op1, reads, writes):
            if op1 is None:
                P.add(eng, lambda e: e.tensor_scalar(out=out, in0=in0, scalar1=s1, scalar2=None, op0=op0), reads=reads, writes=writes)
            else:
                P.add(eng, lambda e: e.tensor_scalar(out=out, in0=in0, scalar1=s1, scalar2=s2, op0=op0, op1=op1), reads=reads, writes=writes)
        def stt(eng, out, in0, scalar, in1, op0, op1, reads, writes):
            P.add(eng, lambda e: e.scalar_tensor_tensor(out=out, in0=in0, scalar=scalar, in1=in1, op0=op0, op1=op1), reads=reads, writes=writes)
        def cpy(eng, out, in_, reads, writes):
            P.add(eng, lambda e: e.tensor_copy(out=out, in_=in_), reads=reads, writes=writes)
        def wdma(out, in_, wb):
            P.add("pool", lambda e: e.dma_start(out=out, in_=in_), writes=[wb], dma=True)

        def gen_sincos(n, posf, posfB, invcol, tmp, cos_o, sin_o, cosB, sinB):
            (ang, angB), (y, yB), (ki, kiB), (r, rB), (m, mB) = tmp
            ts("dve", ang[:, :n], posf, invcol, None, ALU.mult, None, [posfB, cfB], [angB])
            for shift, o, oB in ((0.0, sin_o, sinB), (PI / 2, cos_o, cosB)):
                ts("dve", y[:, :n], ang[:, :n], 1.0 / TWO_PI, shift / TWO_PI, ALU.mult, ALU.add, [angB], [yB])
                cpy("dve", ki[:, :n], y[:, :n], [yB], [kiB])
                cpy("dve", y[:, :n], ki[:, :n], [kiB], [yB])
                if shift:
                    ts("dve", r[:, :n], ang[:, :n], shift, None, ALU.add, None, [angB], [rB])
                    stt("dve", r[:, :n], y[:, :n], -C1, r[:, :n], ALU.mult, ALU.add, [yB, rB], [rB])
                else:
                    stt("dve", r[:, :n], y[:, :n], -C1, ang[:, :n], ALU.mult, ALU.add, [yB, angB], [rB])
                stt("dve", r[:, :n], y[:, :n], -C2, r[:, :n], ALU.mult, ALU.add, [yB, rB], [rB])
                ts("dve", m[:, :n], r[:, :n], PI, -TWO_PI, ALU.is_gt, ALU.mult, [rB], [mB])
                tt("dve", r[:, :n], r[:, :n], m[:, :n], ALU.add, [rB, mB], [rB])
                ts("dve", m[:, :n], r[:, :n], -PI, TWO_PI, ALU.is_lt, ALU.mult, [rB], [mB])
                tt("dve", r[:, :n], r[:, :n], m[:, :n], ALU.add, [rB, mB], [rB])
                act(o, r[:, :n], AF.Sin, [rB], [oB])

        def ln_rows(src, srcB, dst, dstB, gv, gB, bv, bB, st, stB, mv, mvB, width=2048):
            nchk = width // 512
            for k in range(nchk):
                P.add("dve", lambda e, k=k: e.bn_stats(out=st[:, k * 6:(k + 1) * 6], in_=src[:, k * 512:(k + 1) * 512]), reads=[srcB], writes=[stB])
            P.add("dve", lambda e: e.bn_aggr(out=mv[:, 0:2], in_=st[:, 0:nchk * 6]), reads=[stB], writes=[mvB])
            act(mv[:, 2:3], mv[:, 1:2], AF.Sqrt, [mvB, cfB], [mvB], bias=eps_col, scale=1.0)
            P.add("dve", lambda e: e.reciprocal(out=mv[:, 2:3], in_=mv[:, 2:3]), reads=[mvB], writes=[mvB])
            ts("dve", dst, src, mv[:, 0:1], mv[:, 2:3], ALU.subtract, ALU.mult, [srcB, mvB], [dstB])
            tt("dve", dst, dst, gv, ALU.mult, [dstB, gB], [dstB])
            tt("dve", dst, dst, bv, ALU.add, [dstB, bB], [dstB])

        LA = Lay(SIN_OFF)
        SIN, SINB = LA.take([128, 8, 512], F32, "SIN")
        WK = []; WV = []
        for i in range(4): WK.append(LA.take([128, 16, 256], BF16, f"WK{i}"))
        for i in range(8): WV.append(LA.take([128, 16, 256], BF16, f"WV{i}"))
        xb = [LA.take([128, 2, 2048], BF16, f"xb{i}") for i in range(2)]
        xTc = [LA.take([128, 16, 256], BF16, f"xTc{i}") for i in range(2)]
        posi = LA.take([128, 256], I32, "posi"); posf = LA.take([128, 256], F32, "posf")
        tmpA = [LA.take([128, 256], F32 if k != 2 else I32, f"tmpA{k}") for k in range(5)]
        csA = [(LA.take([128, 256], F32, f"cosA{i}"), LA.take([128, 256], F32, f"sinA{i}")) for i in range(2)]
        tuA = [(LA.take([128, 2, 256], F32, f"tA{i}"), LA.take([128, 2, 256], F32, f"uA{i}")) for i in range(2)]
        krA = [LA.take([128, 2, 256], BF16, f"krA{i}") for i in range(2)]
        ktm, ktmB = LA.take([128, 2, 1024], BF16, "ktmA")
        vtm, vtmB = LA.take([128, 2, 2048], BF16, "vtmA")

        P.add("pool", lambda e: e.memset(SIN[:], 0.0), writes=[SINB])
        for i in range(4):
            wdma(WK[i][0][:], wv_in[:, :, 1024 + i * 256: 1024 + (i + 1) * 256], WK[i][1])
        xall_t = xall.rearrange("(c t p) d -> c p t d", p=128, t=2)
        pos_c = pos.rearrange("o (c n) -> c o n", n=256)
        def load_chunk(c):
            s = c % 2
            wdma(xb[s][0][:], xall_t[c], xb[s][1])
        load_chunk(0)
        for i in range(8):
            wdma(WV[i][0][:], wv_in[:, :, 2048 + i * 256: 2048 + (i + 1) * 256], WV[i][1])
        NCHA = NPRI // CH
        for c in range(NCHA):
            s = c % 2
            if c + 1 < NCHA: load_chunk(c + 1)
            xbt, xbB = xb[s]; xT, xTB = xTc[s]
            P.add("sp", lambda e, c=c: e.dma_start(out=posi[0][:], in_=pos_c[c].partition_broadcast(128)), writes=[posi[1]], dma=True)
            cpy("dve", posf[0][:], posi[0][:], [posi[1]], [posf[1]])
            (cosT, cosB), (sinT, sinB) = csA[s]
            gen_sincos(256, posf[0][:], posf[1], cf[:, CF_INVR:CF_INVR + 1], tmpA, cosT[:], sinT[:], cosB, sinB)
            for g in range(4):
                pt, pB = bank(); ptb = pt[:].bitcast(BF16)
                for dcl in range(4):
                    for t2 in range(2):
                        dc = g * 4 + dcl
                        tr(ptb[:, dcl * 256 + t2 * 128: dcl * 256 + (t2 + 1) * 128], xbt[:, t2, dc * 128:(dc + 1) * 128], [xbB], pB)
                act(xT[:, g * 4:(g + 1) * 4, :], ptb.rearrange("p (a n) -> p a n", a=4), AF.Copy, [pB], [xTB])
            for h in range(4):
                pt, pB = bank(); p3 = pt[:].rearrange("p (a n) -> p a n", a=2)
                for half in range(2):
                    col = h * 256 + half * 128
                    wt, wB = WK[col // 256]
                    for dc in range(16):
                        mm(p3[:, half, :], wt[:, dc, col % 256: col % 256 + 128], xT[:, dc, :], dc == 0, dc == 15, [wB, xTB], pB)
                (tT, tB), (uT, uB) = tuA[h % 2]
                krT, krB = krA[h % 2]
                cb3 = cosT[:].unsqueeze(1).to_broadcast([128, 2, 256]); sb3 = sinT[:].unsqueeze(1).to_broadcast([128, 2, 256])
                tt("dve", tT[:], p3, cb3, ALU.mult, [pB, cosB], [tB])
                tt("dve", uT[:], p3, sb3, ALU.mult, [pB, sinB], [uB])
                tt("dve", krT[:, 0, :], tT[:, 0, :], uT[:, 1, :], ALU.subtract, [tB, uB], [krB])
                tt("dve", krT[:, 1, :], tT[:, 1, :], uT[:, 0, :], ALU.add, [tB, uB], [krB])
                pt2, pB2 = bank(); p2b = pt2[:].bitcast(BF16)
                for t2 in range(2):
                    for half in range(2):
                        tr(p2b[:, (t2 * 2 + half) * 128:(t2 * 2 + half + 1) * 128], krT[:, half, t2 * 128:(t2 + 1) * 128], [krB], pB2)
                for t2 in range(2):
                    wcol = cf[:, CF_WPRI + h * 24 + c * 2 + t2: CF_WPRI + h * 24 + c * 2 + t2 + 1]
                    act(ktm[:, t2, h * 256:(h + 1) * 256], p2b[:, t2 * 256:(t2 + 1) * 256], AF.Identity, [pB2, cfB], [ktmB], scale=wcol)
            for t2 in range(2):
                for cbk in range(4):
                    pt, pB = bank()
                    for hf in range(2):
                        wt, wB = WV[cbk * 2 + hf]
                        for dc in range(16):
                            mm(pt[:, hf * 256:(hf + 1) * 256], xT[:, dc, t2 * 128:(t2 + 1) * 128], wt[:, dc, :], dc == 0, dc == 15, [wB, xTB], pB)
                    act(vtm[:, t2, cbk * 512:(cbk + 1) * 512], pt[:], AF.Copy, [pB], [vtmB])
            for h in range(4):
                for dt_ in range(2):
                    pt, pB = bank()
                    for t2 in range(2):
                        mm(pt[:], ktm[:, t2, h * 256 + dt_ * 128: h * 256 + (dt_ + 1) * 128], vtm[:, t2, h * 512:(h + 1) * 512],
                           t2 == 0, t2 == 1, [ktmB, vtmB], pB)
                    tt("dve", SIN[:, h * 2 + dt_, :], SIN[:, h * 2 + dt_, :], pt[:], ALU.add, [SINB, pB], [SINB])

        if stage == "sin":
            obs = []
            for idx in range(8):
                ob = Buf(f"so{idx}"); obs.append(ob)
                P.add("sp", lambda e, idx=idx: e.dma_start(out=dbg_d[idx * 128:(idx + 1) * 128, 0:512], in_=SIN[:, idx, :]), reads=[SINB], writes=[ob], dma=True)
            P.add("sp", lambda e: None, reads=obs)
            sem_list = []
            def sems():
                s_ = es.enter_context(nc.semaphore(f"s{len(sem_list)}")); sem_list.append(s_); return s_
            with nc.Block() as block:
                P.emit(block, sems)
            return nc

        LB = Lay(SIN_OFF + 16384)
        xTa, xTaB = LB.take([128, 16, 1152], BF16, "xTa")
        NWS = 3
        wst = [LB.take([128, 16, 256], BF16, f"wst{i}") for i in range(NWS)]
        retg, retgB = LB.take([128, 8, 2048], BF16, "retg")
        B_COMMON = LB.o
        L1 = Lay(B_COMMON)
        qT, qTB = L1.take([128, 2, 1024], BF16, "qT"); qTs, qTsB = L1.take([128, 2, 1024], BF16, "qTs")
        kT, kTB = L1.take([128, 2, 1024], BF16, "kT"); ktmo, ktmoB = L1.take([128, 8, 256], BF16, "ktmo")
        vtmo, vtmoB = L1.take([128, 8, 512], BF16, "vtmo")
        cosr, cosrB = L1.take([128, 1024], F32, "cosr"); sinr, sinrB = L1.take([128, 1024], F32, "sinr")
        posi1 = L1.take([128, 256], I32, "posi1"); posf1 = L1.take([128, 256], F32, "posf1")
        tmpB = [L1.take([128, 256], F32 if k != 2 else I32, f"tmpB{k}") for k in range(5)]
        rt = [L1.take([128, 512], F32, f"rt{k}") for k in range(4)]
        PTt = [L1.take([128, 2, 256], BF16, f"PT{k}") for k in range(2)]
        stbf, stbfB = L1.take([128, 2, 512], BF16, "stbf")
        gvec, gvB = L1.take([128, 2048], F32, "gvec"); bvec, bvB = L1.take([128, 2048], F32, "bvec")
        ytmp = [L1.take([128, 512], F32, f"ytmp{k}") for k in range(2)]
        stt_t, sttB = L1.take([128, 24], F32, "bnst"); mv_t, mvB = L1.take([128, 4], F32, "mv")
        xb1 = [L1.take([128, 2048], BF16, f"xb1{i}") for i in range(2)]

        xall_t1 = xall.rearrange("(c p) d -> c p d", p=128)
        for t9 in range(9):
            xbt, xbB = xb1[t9 % 2]
            wdma(xbt[:], xall_t1[23 + t9], xbB)
            for g in range(2):
                pt, pB = bank(); ptb = pt[:].bitcast(BF16)
                for dcl in range(8):
                    dc = g * 8 + dcl
                    tr(ptb[:, dcl * 128:(dcl + 1) * 128], xbt[:, dc * 128:(dc + 1) * 128], [xbB], pB)
                act(xTa[:, g * 8:(g + 1) * 8, t9 * 128:(t9 + 1) * 128], ptb.rearrange("p (a n) -> p a n", a=8), AF.Copy, [pB], [xTaB])
        for c4 in range(4):
            P.add("sp", lambda e, c4=c4: e.dma_start(out=posi1[0][:], in_=pos_c[12 + c4].partition_broadcast(128)), writes=[posi1[1]], dma=True)
            cpy("dve", posf1[0][:], posi1[0][:], [posi1[1]], [posf1[1]])
            gen_sincos(256, posf1[0][:], posf1[1], cf[:, CF_INVR:CF_INVR + 1], tmpB,
                       cosr[:, c4 * 256:(c4 + 1) * 256], sinr[:, c4 * 256:(c4 + 1) * 256], cosrB, sinrB)
        P.add("sp", lambda e: e.dma_start(out=gvec[:], in_=vecs["gn_g"].partition_broadcast(128)), writes=[gvB], dma=True)
        P.add("sp", lambda e: e.dma_start(out=bvec[:], in_=vecs["gn_b"].partition_broadcast(128)), writes=[bvB], dma=True)

        if stage == "tab":
            obs = [Buf("tb0"), Buf("tb1")]
            P.add("sp", lambda e: e.dma_start(out=dbg_d[0:128, 0:1024], in_=cosr[:]), reads=[cosrB], writes=[obs[0]], dma=True)
            P.add("sp", lambda e: e.dma_start(out=dbg_d[0:128, 1024:2048], in_=sinr[:]), reads=[sinrB], writes=[obs[1]], dma=True)
            P.add("sp", lambda e: None, reads=obs)
            sem_list = []
            def sems():
                s_ = es.enter_context(nc.semaphore(f"s{len(sem_list)}")); sem_list.append(s_); return s_
            with nc.Block() as block:
                P.emit(block, sems)
            return nc

        wctr = [0]
        def wload(col0):
            wt, wB = wst[wctr[0] % NWS]; wctr[0] += 1
            wdma(wt[:], wv_in[:, :, col0:col0 + 256], wB)
            return wt, wB

        OWN = 128
        def proj_fm_rot(wt, wB, out_bf, outB, out_s, outsB, h):
            for tb in range(2):
                pa, paB = bank(); pb_, pbB = bank()
                for half, (pp, ppB) in enumerate(((pa, paB), (pb_, pbB))):
                    for dc in range(16):
                        mm(pp[:], wt[:, dc, half * 128:(half + 1) * 128], xTa[:, dc, OWN + tb * 512: OWN + (tb + 1) * 512], dc == 0, dc == 15, [wB, xTaB], ppB)
                cs = cosr[:, tb * 512:(tb + 1) * 512]; sn = sinr[:, tb * 512:(tb + 1) * 512]
                (r0, r0B), (r1, r1B), (r2, r2B), (r3, r3B) = rt
                tt("dve", r0[:], pa[:], cs, ALU.mult, [paB, cosrB], [r0B])
                tt("dve", r1[:], pb_[:], sn, ALU.mult, [pbB, sinrB], [r1B])
                tt("dve", r2[:], pb_[:], cs, ALU.mult, [pbB, cosrB], [r2B])
                tt("dve", r3[:], pa[:], sn, ALU.mult, [paB, sinrB], [r3B])
                if out_s is None:
                    tt("dve", out_bf[:, 0, tb * 512:(tb + 1) * 512], r0[:], r1[:], ALU.subtract, [r0B, r1B], [outB])
                    tt("dve", out_bf[:, 1, tb * 512:(tb + 1) * 512], r2[:], r3[:], ALU.add, [r2B, r3B], [outB])
                else:
                    tt("dve", r0[:], r0[:], r1[:], ALU.subtract, [r0B, r1B], [r0B])
                    tt("dve", r2[:], r2[:], r3[:], ALU.add, [r2B, r3B], [r2B])
                    qd = cf[:, CF_QDEC + h * 256: CF_QDEC + (h + 1) * 256].unsqueeze(1).to_broadcast([128, 2, 256])
                    for half, (rr, rrB) in enumerate(((r0, r0B), (r2, r2B))):
                        act(out_bf[:, half, tb * 512:(tb + 1) * 512], rr[:], AF.Copy, [rrB], [outB])
                        tt("dve", out_s[:, half, tb * 512:(tb + 1) * 512].rearrange("p (a n) -> p a n", a=2),
                           rr[:].rearrange("p (a n) -> p a n", a=2), qd, ALU.mult, [rrB, cfB], [outsB])

        for h in range(4):
            wq = wload(h * 256); wk = wload(1024 + h * 256)
            proj_fm_rot(wq[0], wq[1], qT, qTB, qTs, qTsB, h)
            wv0 = wload(2048 + h * 512)
            proj_fm_rot(wk[0], wk[1], kT, kTB, None, None, h)
            wv1 = wload(2048 + h * 512 + 256)
            for t8 in range(8):
                pt, pB = bank(); ptb = pt[:].bitcast(BF16)
                for half in range(2):
                    tr(ptb[:, half * 128:(half + 1) * 128], kT[:, half, t8 * 128:(t8 + 1) * 128], [kTB], pB)
                kcol = cf[:, CF_KDEC + h * 2 + t8 % 2: CF_KDEC + h * 2 + t8 % 2 + 1]
                act(ktmo[:, t8, :], ptb[:, 0:256], AF.Identity, [pB, cfB], [ktmoB], scale=kcol)
            for t8 in range(8):
                pt, pB = bank()
                for hf, (wt, wB) in enumerate((wv0, wv1)):
                    for dc in range(16):
                        mm(pt[:, hf * 256:(hf + 1) * 256], xTa[:, dc, OWN + t8 * 128: OWN + (t8 + 1) * 128], wt[:, dc, :], dc == 0, dc == 15, [wB, xTaB], pB)
                act(vtmo[:, t8, :], pt[:], AF.Copy, [pB], [vtmoB])
            for i in range(4):
                for dt_ in range(2):
                    act(stbf[:, dt_, :], SIN[:, h * 2 + dt_, :], AF.Copy, [SINB], [stbfB])
                PT, PTB = PTt[i % 2]
                for mt in range(2):
                    c0 = mt * 128
                    pt, pB = bank()
                    for dh in range(2):
                        mm(pt[:, c0:256], kT[:, dh, i * 256 + mt * 128: i * 256 + (mt + 1) * 128], qT[:, dh, i * 256 + c0: i * 256 + 256],
                           dh == 0, dh == 1, [kTB, qTB], pB)
                    dm = cf[:, CF_DM + (h * 2 + mt) * 256 + c0: CF_DM + (h * 2 + mt + 1) * 256]
                    tt("dve", PT[:, mt, c0:256], pt[:, c0:256], dm, ALU.mult, [pB, cfB], [PTB])
                for ct in range(2):
                    pt, pB = bank()
                    seq = [(PT[:, 0, ct * 128:(ct + 1) * 128], vtmo[:, i * 2, :], [PTB, vtmoB])]
                    if ct == 1:
                        seq.append((PT[:, 1, 128:256], vtmo[:, i * 2 + 1, :], [PTB, vtmoB]))
                    for dt_ in range(2):
                        seq.append((qTs[:, dt_, i * 256 + ct * 128: i * 256 + (ct + 1) * 128], stbf[:, dt_, :], [qTsB, stbfB]))
                    for k, (l_, r_, rd) in enumerate(seq):
                        mm(pt[:], l_, r_, k == 0, k == len(seq) - 1, rd, pB)
                    yt, ytB = ytmp[ct]
                    P.add("dve", lambda e, pt=pt: e.bn_stats(out=stt_t[:, 0:6], in_=pt[:]), reads=[pB], writes=[sttB])
                    P.add("dve", lambda e: e.bn_aggr(out=mv_t[:, 0:2], in_=stt_t[:, 0:6]), reads=[sttB], writes=[mvB])
                    act(mv_t[:, 2:3], mv_t[:, 1:2], AF.Sqrt, [mvB, cfB], [mvB], bias=eps_col, scale=1.0)
                    P.add("dve", lambda e: e.reciprocal(out=mv_t[:, 2:3], in_=mv_t[:, 2:3]), reads=[mvB], writes=[mvB])
                    ts("dve", yt[:], pt[:], mv_t[:, 0:1], mv_t[:, 2:3], ALU.subtract, ALU.mult, [pB, mvB], [ytB])
                    tt("dve", yt[:], yt[:], gvec[:, h * 512:(h + 1) * 512], ALU.mult, [ytB, gvB], [ytB])
                    tt("dve", retg[:, i * 2 + ct, h * 512:(h + 1) * 512], yt[:], bvec[:, h * 512:(h + 1) * 512], ALU.add, [ytB, bvB], [retgB])
                if i < 3:
                    for dt_ in range(2):
                        pt, pB = bank()
                        for t2 in range(2):
                            mm(pt[:], ktmo[:, i * 2 + t2, dt_ * 128:(dt_ + 1) * 128], vtmo[:, i * 2 + t2, :], t2 == 0, t2 == 1, [ktmoB, vtmoB], pB)
                        stt("dve", SIN[:, h * 2 + dt_, :], SIN[:, h * 2 + dt_, :], GAM_CD[h], pt[:], ALU.mult, ALU.add, [SINB, pB], [SINB])

        def dump_tokmajor_bf(src, srcB, lay_off):
            LD = Lay(lay_off)
            dts = [LD.take([128, 2048], F32, f"dbgt{k}") for k in range(2)]
            obs = []
            for t8 in range(8):
                dt_, dB = dts[t8 % 2]
                cpy("dve", dt_[:], src[:, t8, :], [srcB], [dB])
                ob = Buf(f"dbgo{t8}"); obs.append(ob)
                P.add("sp", lambda e, t8=t8, dt_=dt_: e.dma_start(out=dbg_d[t8 * 128:(t8 + 1) * 128, :], in_=dt_[:]), reads=[dB], writes=[ob], dma=True)
            P.add("sp", lambda e: None, reads=obs)

        def finish():
            sem_list = []
            def sems():
                s = es.enter_context(nc.semaphore(f"s{len(sem_list)}")); sem_list.append(s); return s
            with nc.Block() as block:
                P.emit(block, sems)
            return nc

        def zero_out_and_finish(lay_off):
            LZ = Lay(lay_off)
            zt, zB = LZ.take([128, 2048], F32, "zt")
            P.add("pool", lambda e: e.memset(zt[:], 0.0), writes=[zB])
            obs = []
            for t8 in range(8):
                ob = Buf(f"zo{t8}"); obs.append(ob)
                P.add("sp", lambda e, t8=t8: e.dma_start(out=out_d[t8 * 128:(t8 + 1) * 128, :], in_=zt[:]), reads=[zB], writes=[ob], dma=True)
            P.add("sp", lambda e: None, reads=obs)
            return finish()
        if stage == "ret":
            dump_tokmajor_bf(retg, retgB, SIN_OFF)
            return zero_out_and_finish(SIN_OFF + 16384)

        L2 = Lay(B_COMMON)
        attT, attTB = L2.take([128, 16, 1024], BF16, "attT")
        akT, akTB = L2.take([128, 4, 1152], BF16, "akT")
        avd, avdB = L2.take([128, 9, 512], BF16, "avd")
        cosa, cosaB = L2.take([128, 1152], F32, "cosa"); sina, sinaB = L2.take([128, 1152], F32, "sina")
        L2T = Lay(L2.o)
        posi2 = L2T.take([128, 384], I32, "posi2"); posf2 = L2T.take([128, 384], F32, "posf2")
        tmpC = [L2T.take([128, 384], F32 if k != 2 else I32, f"tmpC{k}") for k in range(5)]
        try:
            for c3 in range(3):
                P.add("sp", lambda e, c3=c3: e.dma_start(out=posi2[0][:], in_=pos[:, NPRI - 128 + c3 * 384: NPRI - 128 + (c3 + 1) * 384].partition_broadcast(128)),
                      writes=[posi2[1]], dma=True)
                cpy("dve", posf2[0][:], posi2[0][:], [posi2[1]], [posf2[1]])
                gen_sincos(384, posf2[0][:], posf2[1], cf[:, CF_INVA:CF_INVA + 1], tmpC,
                           cosa[:, c3 * 384:(c3 + 1) * 384], sina[:, c3 * 384:(c3 + 1) * 384], cosaB, sinaB)

            aqT = [L2.take([128, 4, 1024], BF16, "aqT0")] * 2
            qb_bf = [L2.take([128, 512], BF16, f"qbbf{k}") for k in range(2)]
            ra = [L2.take([128, 512], F32, "ra0")] * 2
            rb = [L2.take([128, 512], F32, "rb0")] * 2
            PTa = [L2.take([128, 4, 512], BF16, f"PTa{k}") for k in range(2)]
            den = [L2.take([128, 512], F32, "den0")] * 2

            if cut == 1: raise _Cut()
            def att_rot(pp, ppB, n, tok0, out_ap, outB, k):
                qbt, qbB = qb_bf[k % 2]; (a_, aB) = ra[k % 2]; (b_, bB) = rb[k % 2]
                cpy("dve", qbt[:, :n], pp[:, :n], [ppB], [qbB])
                tt("dve", a_[:, :n], pp[:, :n], cosa[:, tok0:tok0 + n], ALU.mult, [ppB, cosaB], [aB])
                p2, p2B = bank()
                mm(p2[:, :n], Rm, qbt[:, :n], True, True, [cbB, qbB], p2B)
                tt("dve", b_[:, :n], p2[:, :n], sina[:, tok0:tok0 + n], ALU.mult, [p2B, sinaB], [bB])
                tt("dve", out_ap, a_[:, :n], b_[:, :n], ALU.add, [aB, bB], [outB])

            AK0 = 8192; AV0 = 8448; AQ0 = 6144
            cnt = 0
            wk_, wkB = wload(AK0)
            for ch2 in range(2):
                for (t0_, n_) in ((0, 512), (512, 512), (1024, 128)):
                    pt, pB = bank()
                    for dc in range(16):
                        mm(pt[:, :n_], wk_[:, dc, ch2 * 128:(ch2 + 1) * 128], xTa[:, dc, t0_:t0_ + n_], dc == 0, dc == 15, [wkB, xTaB], pB)
                    att_rot(pt, pB, n_, t0_, akT[:, ch2, t0_:t0_ + n_], akTB, cnt); cnt += 1
                    p2, p2B = bank()
                    mm(p2[:, :n_], SWm, akT[:, ch2, t0_:t0_ + n_], True, True, [cbB, akTB], p2B)
                    act(akT[:, 2 + ch2, t0_:t0_ + n_], p2[:, :n_], AF.Copy, [p2B], [akTB])
            if cut == 2: raise _Cut()
            wv_, wvB = wload(AV0)
            for t9 in range(9):
                pt, pB = bank()
                for dc in range(16):
                    mm(pt[:, :256], xTa[:, dc, t9 * 128:(t9 + 1) * 128], wv_[:, dc, :], dc == 0, dc == 15, [wvB, xTaB], pB)
                for r_ in range(2):
                    act(avd[:, t9, :].rearrange("p (k r d) -> p k r d", k=4, r=2)[:, :, r_, :], pt[:, :256].rearrange("p (k d) -> p k d", k=4), AF.Copy, [pB], [avdB])
            if cut == 3: raise _Cut()
            for kvg in range(4):
                aq, aqB = aqT[kvg % 2]
                for blk in range(2):
                    wt, wB = wload(AQ0 + kvg * 512 + blk * 256)
                    for cl in range(2):
                        for tb in range(2):
                            pt, pB = bank()
                            for dc in range(16):
                                mm(pt[:], wt[:, dc, cl * 128:(cl + 1) * 128], xTa[:, dc, OWN + tb * 512: OWN + (tb + 1) * 512], dc == 0, dc == 15, [wB, xTaB], pB)
                            att_rot(pt, pB, 512, OWN + tb * 512, aq[:, blk * 2 + cl, tb * 512:(tb + 1) * 512], aqB, cnt); cnt += 1
                if cut == 4: raise _Cut()
                for qb in range(8):
                    PT, PTB = PTa[qb % 2]
                    for kt in range(2):
                        keyt = qb + kt
                        if kt == 1: mcol = CB_MCUR
                        else: mcol = CB_MPREV0 if qb == 0 else CB_MPREV
                        for par in range(2):
                            pt, pB = bank()
                            mm(pt[:], ident, cb[:, mcol:mcol + 512], True, False, [cbB], pB)
                            for a4 in range(4):
                                mm(pt[:, a4 * 128:(a4 + 1) * 128], akT[par * 64:(par + 1) * 64, (kvg // 2) + (0 if kvg % 2 == par else 2), keyt * 128:(keyt + 1) * 128],
                                   aq[par * 64:(par + 1) * 64, a4, qb * 128:(qb + 1) * 128], False, a4 == 3, [akTB, aqB], pB)
                            act(PT[:, kt * 2 + par, :], pt[:], AF.Exp, [pB], [PTB], scale=0.125)
                    if cut == 5: raise _Cut()
                    for par in range(2):
                        po, poB = bank(); pd, pdB = bank()
                        for kt in range(2):
                            mm(po[:], avd[:, qb + kt, kvg * 128:(kvg + 1) * 128], PT[:, kt * 2 + par, :], kt == 0, kt == 1, [avdB, PTB], poB)
                        for kt in range(2):
                            mm(pd[:], ones, PT[:, kt * 2 + par, :], kt == 0, kt == 1, [cbB, PTB], pdB)
                        if cut == 6: raise _Cut()
                        dn, dnB = den[par]
                        esb = es_all[:, kvg * 8 + par: kvg * 8 + 8: 2].unsqueeze(2).to_broadcast([128, 4, 128])
                        tt("dve", dn[:].rearrange("p (a n) -> p a n", a=4), pd[:].rearrange("p (a n) -> p a n", a=4), esb, ALU.add, [pdB, esB], [dnB])
                        P.add("dve", lambda e, dn=dn: e.reciprocal(out=dn[:], in_=dn[:]), reads=[dnB], writes=[dnB])
                        lo = par * 64
                        tt("dve", attT[lo:lo + 64, kvg * 4:(kvg + 1) * 4, qb * 128:(qb + 1) * 128],
                           po[lo:lo + 64, :].rearrange("p (a n) -> p a n", a=4), dn[lo:lo + 64, :].rearrange("p (a n) -> p a n", a=4),
                           ALU.mult, [poB, dnB], [attTB])


        except _Cut:
            dump_tokmajor_bf(retg, retgB, SIN_OFF)
            return zero_out_and_finish(SIN_OFF + 16384)
        def dump_featmajor(src, srcB, lay_off):
            LDm = Lay(lay_off)
            dts = [LDm.take([128, 1024], F32, f"dfm{k}") for k in range(2)]
            obs = []
            for ch in range(16):
                dt_, dB = dts[ch % 2]
                cpy("dve", dt_[:], src[:, ch, :], [srcB], [dB])
                ob = Buf(f"dfo{ch}"); obs.append(ob)
                P.add("sp", lambda e, ch=ch, dt_=dt_: e.dma_start(out=dbg_d[ch * 64:(ch + 1) * 64, :].rearrange("r (two n) -> (r two) n", two=2), in_=dt_[:]),
                      reads=[dB], writes=[ob], dma=True)
            P.add("sp", lambda e: None, reads=obs)
        if stage == "att":
            dump_featmajor(attT, attTB, SIN_OFF)
            return zero_out_and_finish(SIN_OFF + 8192)

        L3 = Lay(B_COMMON + 32768)
        g1 = [L3.take([128, 256], F32, f"g1{k}") for k in range(2)]
        g2 = [L3.take([128, 256], F32, f"g2{k}") for k in range(2)]
        g3 = [L3.take([128, 512], F32, f"g3{k}") for k in range(2)]
        RG0 = 4096; GA0 = 8704; GB0 = 10752
        for blk in range(8):
            wg = wload(RG0 + blk * 256); wa = wload(GA0 + blk * 256)
            for t8 in range(8):
                pg, pgB = bank(); pa, paB = bank()
                for (pp, ppB), (wt, wB) in (((pg, pgB), wg), ((pa, paB), wa)):
                    for dc in range(16):
                        mm(pp[:, :256], xTa[:, dc, OWN + t8 * 128: OWN + (t8 + 1) * 128], wt[:, dc, :], dc == 0, dc == 15, [wB, xTaB], ppB)
                (t1, t1B) = g1[t8 % 2]; (t2_, t2B) = g2[t8 % 2]
                act(t1[:], pg[:, :256], AF.Silu, [pgB], [t1B])
                act(t2_[:], pa[:, :256], AF.Sigmoid, [paB], [t2B])
                tt("dve", t1[:], t1[:], t2_[:], ALU.mult, [t1B, t2B], [t1B])
                tt("dve", retg[:, t8, blk * 256:(blk + 1) * 256], retg[:, t8, blk * 256:(blk + 1) * 256], t1[:], ALU.mult, [retgB, t1B], [retgB])
        for blk in range(8):
            wb_ = wload(GB0 + blk * 256)
            for cl in range(2):
                for tb in range(2):
                    pt, pB = bank()
                    for dc in range(16):
                        mm(pt[:], wb_[0][:, dc, cl * 128:(cl + 1) * 128], xTa[:, dc, OWN + tb * 512: OWN + (tb + 1) * 512], dc == 0, dc == 15, [wb_[1], xTaB], pB)
                    (t3, t3B) = g3[(cl * 2 + tb) % 2]
                    act(t3[:], pt[:], AF.Sigmoid, [pB], [t3B])
                    ch_ = blk * 2 + cl
                    tt("dve", attT[:, ch_, tb * 512:(tb + 1) * 512], attT[:, ch_, tb * 512:(tb + 1) * 512], t3[:], ALU.mult, [attTB, t3B], [attTB])
        for t8 in range(8):
            for g in range(2):
                pt, pB = bank(); ptb = pt[:].bitcast(BF16)
                for dcl in range(8):
                    dc = g * 8 + dcl
                    tr(ptb[:, dcl * 128:(dcl + 1) * 128], retg[:, t8, dc * 128:(dc + 1) * 128], [retgB], pB)
                tt("dve", attT[:, g * 8:(g + 1) * 8, t8 * 128:(t8 + 1) * 128], attT[:, g * 8:(g + 1) * 8, t8 * 128:(t8 + 1) * 128],
                   ptb.rearrange("p (a n) -> p a n", a=8), ALU.add, [attTB, pB], [attTB])
        mT, mTB = attT, attTB
        if stage == "merged":
            dump_featmajor(attT, attTB, SIN_OFF)
            return zero_out_and_finish(SIN_OFF + 8192)

        LC = Lay(SIN_OFF)
        acc = [LC.take([128, 2048], F32, f"acc{t8}") for t8 in range(8)]
        assert LC.o <= B_COMMON, (LC.o, B_COMMON)
        CFREE = LC.o
        LCa = Lay(CFREE)
        gv1, gv1B = LCa.take([128, 2048], F32, "gv1"); bv1, bv1B = LCa.take([128, 2048], F32, "bv1")
        st1, st1B = LCa.take([128, 24], F32, "st1"); mv1, mv1B = LCa.take([128, 4], F32, "mv1")
        wso = [LCa.take([128, 16, 256], BF16, f"wso{i}") for i in range(3)]
        assert LCa.o <= B_COMMON, (LCa.o, B_COMMON)
        LCb = Lay(B_COMMON + 32768)
        h1T, h1TB = LCb.take([128, 16, 1024], BF16, "h1T")
        H1T_END = LCb.o
        h1b = [LCb.take([128, 2048], BF16, f"h1b{k}") for k in range(2)]
        for t8 in range(8):
            P.add("sp", lambda e, t8=t8: e.dma_start(out=acc[t8][0][:], in_=xall[NPRI + t8 * 128: NPRI + (t8 + 1) * 128, :]), writes=[acc[t8][1]], dma=True)
        P.add("sp", lambda e: e.dma_start(out=gv1[:], in_=vecs["ln1_g"].partition_broadcast(128)), writes=[gv1B], dma=True)
        P.add("sp", lambda e: e.dma_start(out=bv1[:], in_=vecs["ln1_b"].partition_broadcast(128)), writes=[bv1B], dma=True)
        wv_out = w_out.rearrange("(c p) n -> p c n", p=128)
        woc = [0]
        def wload_o(blk):
            wt, wB = wso[woc[0] % 3]; woc[0] += 1
            wdma(wt[:], wv_out[:, :, blk * 256:(blk + 1) * 256], wB)
            return wt, wB
        pend = [wload_o(0), wload_o(1)]
        for blk in range(8):
            wt, wB = pend.pop(0)
            if blk + 2 < 8: pend.append(wload_o(blk + 2))
            for t8 in range(8):
                pt, pB = bank()
                for dc in range(16):
                    mm(pt[:, :256], mT[:, dc, t8 * 128:(t8 + 1) * 128], wt[:, dc, :], dc == 0, dc == 15, [wB, mTB], pB)
                a_, aB = acc[t8]
                stt("dve", a_[:, blk * 256:(blk + 1) * 256], a_[:, blk * 256:(blk + 1) * 256], ALPHA, pt[:, :256], ALU.mult, ALU.add, [aB, pB], [aB])
        for t8 in range(8):
            a_, aB = acc[t8]
            ln_rows(a_[:], aB, a_[:], aB, gv1[:], gv1B, bv1[:], bv1B, st1, st1B, mv1, mv1B)
            hb, hbB = h1b[t8 % 2]
            act(hb[:], a_[:], AF.Copy, [aB], [hbB])
            for g in range(2):
                pt, pB = bank(); ptb = pt[:].bitcast(BF16)
                for dcl in range(8):
                    dc = g * 8 + dcl
                    tr(ptb[:, dcl * 128:(dcl + 1) * 128], hb[:, dc * 128:(dc + 1) * 128], [hbB], pB)
                act(h1T[:, g * 8:(g + 1) * 8, t8 * 128:(t8 + 1) * 128], ptb.rearrange("p (a n) -> p a n", a=8), AF.Copy, [pB], [h1TB])
            if stage == "h1":
                P.add("sp", lambda e, t8=t8: e.dma_start(out=dbg_d[t8 * 128:(t8 + 1) * 128, :], in_=acc[t8][0][:]), reads=[aB], dma=True)
            act(a_[:], a_[:], AF.Identity, [aB], [aB], scale=ALPHA)

        LD_ = Lay(B_COMMON)
        wgu = [LD_.take([128, 16, 256], BF16, f"wgu{i}") for i in range(4)]
        assert LD_.o <= B_COMMON + 32768
        LDa = Lay(CFREE)
        wdn = [LDa.take([128, 4, 2048], BF16, f"wdn{k}") for k in range(2)]
        aTg = [LDa.take([128, 4, 1024], BF16, "aTg0")]
        sgt = [LDa.take([128, 512], F32, f"sgt{k}") for k in range(2)]
        assert LDa.o <= B_COMMON, (LDa.o, B_COMMON)
        LDc = Lay(H1T_END)
        aTg.append(LDc.take([128, 4, 1024], BF16, "aTg1"))
        wv_g = w_gate.rearrange("(c p) n -> p c n", p=128); wv_u = w_up.rearrange("(c p) n -> p c n", p=128)
        wv_d = w_down.rearrange("(g c p) n -> g p c n", p=128, c=4)
        NFG = DFF // 512
        guc = [0]
        def wload_gu(fg, blk):
            r = []
            for src in (wv_g, wv_u):
                wt, wB = wgu[guc[0] % 4]; guc[0] += 1
                wdma(wt[:], src[:, :, fg * 512 + blk * 256: fg * 512 + (blk + 1) * 256], wB)
                r.append((wt, wB))
            return r
        seqb = [(fg, blk) for fg in range(NFG) for blk in range(2)]
        pend = [wload_gu(*seqb[0])]
        for si, (fg, blk) in enumerate(seqb):
            if blk == 0:
                wd, wdB = wdn[fg % 2]
                wdma(wd[:], wv_d[fg], wdB)
            (wg_, wgB), (wu_, wuB) = pend.pop(0)
            if si + 1 < len(seqb): pend.append(wload_gu(*seqb[si + 1]))
            aT, aTB = aTg[fg % 2]
            for cl in range(2):
                fc = blk * 2 + cl
                for tb in range(2):
                    pg, pgB = bank(); pu, puB = bank()
                    for (pp, ppB), (wt, wB) in (((pg, pgB), (wg_, wgB)), ((pu, puB), (wu_, wuB))):
                        for dc in range(16):
                            mm(pp[:], wt[:, dc, cl * 128:(cl + 1) * 128], h1T[:, dc, tb * 512:(tb + 1) * 512], dc == 0, dc == 15, [wB, h1TB], ppB)
                    sg, sgB = sgt[(cl * 2 + tb) % 2]
                    act(sg[:], pg[:], AF.Silu, [pgB], [sgB])
                    tt("dve", aT[:, fc, tb * 512:(tb + 1) * 512], sg[:], pu[:], ALU.mult, [sgB, puB], [aTB])
            if blk == 1:
                wd, wdB = wdn[fg % 2]
                for t8 in range(8):
                    a_, aB = acc[t8]
                    for cbk in range(4):
                        pt, pB = bank()
                        for fc in range(4):
                            mm(pt[:], aT[:, fc, t8 * 128:(t8 + 1) * 128], wd[:, fc, cbk * 512:(cbk + 1) * 512], fc == 0, fc == 3, [aTB, wdB], pB)
                        tt("dve", a_[:, cbk * 512:(cbk + 1) * 512], a_[:, cbk * 512:(cbk + 1) * 512], pt[:], ALU.add, [aB, pB], [aB])

        LE = Lay(B_COMMON)
        gv2, gv2B = LE.take([128, 2048], F32, "gv2"); bv2, bv2B = LE.take([128, 2048], F32, "bv2")
        st2, st2B = LE.take([128, 24], F32, "st2"); mv2, mv2B = LE.take([128, 4], F32, "mv2")
        P.add("sp", lambda e: e.dma_start(out=gv2[:], in_=vecs["ln2_g"].partition_broadcast(128)), writes=[gv2B], dma=True)
        P.add("sp", lambda e: e.dma_start(out=bv2[:], in_=vecs["ln2_b"].partition_broadcast(128)), writes=[bv2B], dma=True)
        outBs = [Buf(f"outd{t8}") for t8 in range(8)]
        for t8 in range(8):
            a_, aB = acc[t8]
            ln_rows(a_[:], aB, a_[:], aB, gv2[:], gv2B, bv2[:], bv2B, st2, st2B, mv2, mv2B)
            P.add("sp", lambda e, t8=t8: e.dma_start(out=out_d[t8 * 128:(t8 + 1) * 128, :], in_=acc[t8][0][:]), reads=[aB], writes=[outBs[t8]], dma=True)
        P.add("sp", lambda e: None, reads=outBs)
        return finish()


def _core_inputs(inputs, c):
    b, j = divmod(c, 4)
    x = np.asarray(inputs["x"]); positions = np.asarray(inputs["positions"])
    start = 1024 * j
    xall = np.zeros((NALL, D), np.float32); posall = np.zeros((1, NALL), np.int32)
    npri = min(start, NPRI)
    if npri:
        xall[NPRI - npri:NPRI] = x[b, start - npri:start]
        posall[0, NPRI - npri:NPRI] = positions[b, start - npri:start]
    xall[NPRI:] = x[b, start:start + T]
    posall[0, NPRI:] = positions[b, start:start + T]
    cf, cb = _const_tables(j)
    m = {"xall": xall, "pos": posall, "cf": cf, "cb": cb,
         "w_in": np.ascontiguousarray(inputs["w_in"][0]), "w_out": np.ascontiguousarray(inputs["w_out"][0]),
         "w_gate": np.ascontiguousarray(inputs["w_gate"][0]), "w_up": np.ascontiguousarray(inputs["w_up"][0]),
         "w_down": np.ascontiguousarray(inputs["w_down"][0]),
         "sinks": np.ascontiguousarray(inputs["att_sinks"]).reshape(1, 32).astype(np.float32)}
    for n, k in (("gn_g", "ret_gn_g"), ("gn_b", "ret_gn_b"), ("ln1_g", "ln1_g"), ("ln1_b", "ln1_b"), ("ln2_g", "ln2_g"), ("ln2_b", "ln2_b")):
        m[n] = np.ascontiguousarray(inputs[k]).reshape(1, D).astype(np.float32)
    return m


def kernel(**inputs):
    nc = build("full")
    in_maps = [_core_inputs(inputs, c) for c in range(8)]
    res = run_bass_kernel_spmd(nc, in_maps, core_ids=list(range(8)))
    out = np.zeros((2, SEQ, D), np.float32)
    for c in range(8):
        b, j = divmod(c, 4)
        out[b, 1024 * j:1024 * (j + 1)] = res.results[c]["out"]
    return out
```

```python
import contextlib
import numpy as np
import ml_dtypes
import concourse.bass as bass
import concourse.mybir as mybir
from concourse.bass_utils import run_bass_kernel_spmd

F32 = mybir.dt.float32; BF16 = mybir.dt.bfloat16; I32 = mybir.dt.int32
AF = mybir.ActivationFunctionType; ALU = mybir.AluOpType

D = 2048; SEQ = 4096; T = 1024; NPRI = 3072; NALL = 4096
DFF = 5632
RH = 4; RDK = 256; RDV = 512; CH = 256
ALPHA = 2.0 ** 0.25
PI = float(np.pi); TWO_PI = float(2 * np.pi); C1 = 6.28125; C2 = float(2 * np.pi - 6.28125)
SB_BASE = 16640; SB_LIMIT = 229376
NEG = -30000.0
STRICT_SAME_ENGINE = True

class Buf:
    __slots__ = ("name", "writers", "readers", "sem", "semcnt", "lo", "hi")
    def __init__(self, name, lo=None, hi=None):
        self.name = name; self.writers = {}; self.readers = {}; self.sem = None; self.semcnt = 0
        self.lo = lo; self.hi = hi

class Op:
    __slots__ = ("eng", "fn", "deps", "sig", "cnt", "idx", "dma", "sembuf", "dval", "gi")
    def __init__(self, eng, fn, dma, gi):
        self.eng = eng; self.fn = fn; self.deps = []; self.sig = False; self.cnt = None
        self.idx = None; self.dma = dma; self.sembuf = None; self.dval = None; self.gi = gi

ENGS = ("pe", "act", "dve", "pool", "sp")

class Prog:
    def __init__(self):
        self.ops = {e: [] for e in ENGS}
        self.nops = 0
        self.live = []
    def region(self, name, lo, hi):
        nb = Buf(name, lo, hi)
        keep = []
        for b in self.live:
            if b.lo < hi and lo < b.hi:
                for dct_new, dct_old in ((nb.writers, b.writers), (nb.readers, b.readers)):
                    for k, o in dct_old.items():
                        if k not in dct_new or dct_new[k].gi < o.gi:
                            dct_new[k] = o
                for rlo, rhi in ((b.lo, lo), (hi, b.hi)):
                    if rlo < rhi:
                        rb = Buf(b.name + "_rem", rlo, rhi)
                        rb.writers = dict(b.writers); rb.readers = dict(b.readers)
                        keep.append(rb)
            else:
                keep.append(b)
        keep.append(nb)
        self.live = keep
        return nb
    def add(self, eng, fn, reads=(), writes=(), dma=False, sembuf=None):
        op = Op(eng, fn, dma, self.nops)
        key = ("dma", self.nops) if dma else eng
        self.nops += 1
        deps = {}; raw = set()
        for b in reads:
            for d in b.writers.values():
                deps[id(d)] = d; raw.add(id(d))
        for b in writes:
            for d in b.writers.values(): deps[id(d)] = d
            for d in b.readers.values(): deps[id(d)] = d
        for i, d in deps.items():
            if (not d.dma) and d.eng == eng:
                if eng == "pe" or (not STRICT_SAME_ENGINE and i not in raw):
                    continue
            op.deps.append(d)
            if not d.dma: d.sig = True
        for b in reads: b.readers[key] = op
        for b in writes:
            b.writers = {key: op}; b.readers = {}
        if dma:
            sb = sembuf if sembuf is not None else (writes[0] if writes else reads[0])
            op.sembuf = sb; sb.semcnt += 16; op.dval = sb.semcnt
        op.idx = len(self.ops[eng]); self.ops[eng].append(op)
        return op
    def emit(self, block, sems):
        esem = {e: sems() for e in ENGS}
        for e in ENGS:
            c = 0
            for op in self.ops[e]:
                if op.dma:
                    if op.sembuf.sem is None: op.sembuf.sem = sems()
                elif op.sig:
                    c += 1; op.cnt = c
        def body(e_name):
            def _f(e):
                waited = {}; wdma = {}
                for op in self.ops[e_name]:
                    for d in op.deps:
                        if d.dma:
                            sb = d.sembuf
                            if wdma.get(id(sb), 0) < d.dval:
                                e.wait_ge(sb.sem, d.dval); wdma[id(sb)] = d.dval
                        else:
                            if waited.get(d.eng, -1) < d.idx:
                                e.wait_ge(esem[d.eng], d.cnt); waited[d.eng] = d.idx
                    ins = op.fn(e)
                    if ins is None: continue
                    if op.dma: ins.then_inc(op.sembuf.sem, 16)
                    elif op.sig: ins.then_inc(esem[e_name], 1)
            return _f
        block.tensor(body("pe")); block.scalar(body("act")); block.vector(body("dve"))
        block.gpsimd(body("pool")); block.sync(body("sp"))

NCF = 1 + 1 + 8 + 96 + 2048 + 1024 + 4
CF_INVR = 0; CF_INVA = 1; CF_KDEC = 2; CF_WPRI = 10; CF_DM = 106; CF_QDEC = 106 + 2048; CF_MISC = 106 + 2048 + 1024
NCB = 128 * 4 + 512 * 3
CB_ID = 0; CB_RM = 128; CB_ONES = 256; CB_MCUR = 384; CB_MPREV = 896; CB_MPREV0 = 1408; CB_SW = 1920

def _const_tables(j):
    cf = np.zeros((128, NCF), np.float32)
    p = np.arange(128)
    cf[:, CF_INVR] = (1.0 / (np.float32(10000.0) ** np.linspace(0.0, 1.0, 128, dtype=np.float32))).astype(np.float32)
    inva = (1.0 / (np.float32(500000.0) ** (np.arange(0, 16, 2, dtype=np.float32) / np.float32(16)))).astype(np.float32)
    pm = p % 64
    cf[:, CF_INVA] = np.where(pm < 16, inva[pm % 8], 0.0)
    gam = 1.0 - np.exp2(-5.0 - np.arange(4, dtype=np.float64))
    lg = np.log(gam)
    for h in range(4):
        for mt in range(2):
            m = mt * 128 + p
            cf[:, CF_KDEC + h * 2 + mt] = np.exp(lg[h] * (255.0 - m))
            c = np.arange(256)
            diff = c[None, :] - m[:, None]
            cf[:, CF_DM + (h * 2 + mt) * 256: CF_DM + (h * 2 + mt + 1) * 256] = \
                np.where(diff >= 0, np.exp(lg[h] * np.maximum(diff, 0)), 0.0) * (RDK ** -0.5)
        for tl in range(24):
            m = tl * 128 + p
            valid = m >= (NPRI - 1024 * j)
            cf[:, CF_WPRI + h * 24 + tl] = np.where(valid, np.exp(lg[h] * (NPRI - 1.0 - m)), 0.0)
        cf[:, CF_QDEC + h * 256: CF_QDEC + (h + 1) * 256] = (np.exp(lg[h] * (np.arange(256) + 1.0)) * (RDK ** -0.5))[None, :]
    cf[:, CF_MISC] = 1e-5
    cb = np.zeros((128, NCB), np.float32)
    cb[:, CB_ID:CB_ID + 128] = np.eye(128)
    rm = np.zeros((128, 128), np.float32)
    for m in range(128):
        mm = m % 64
        if mm < 8: rm[m + 8, m] = -1.0
        elif mm < 16: rm[m - 8, m] = 1.0
    cb[:, CB_RM:CB_RM + 128] = rm
    cb[:, CB_ONES:CB_ONES + 128] = 1.0
    for m in range(128):
        cb[(m + 64) % 128, CB_SW + m] = 1.0
    kk = p[:, None]; qq = np.arange(128)[None, :]
    mcur = np.where(kk <= qq, 0.0, NEG); mprev = np.where(kk > qq, 0.0, NEG)
    mprev0 = mprev if j > 0 else np.full((128, 128), NEG)
    cb[:, CB_MCUR:CB_MCUR + 512] = np.tile(mcur, (1, 4))
    cb[:, CB_MPREV:CB_MPREV + 512] = np.tile(mprev, (1, 4))
    cb[:, CB_MPREV0:CB_MPREV0 + 512] = np.tile(mprev0, (1, 4))
    return cf, cb.astype(ml_dtypes.bfloat16)

GAM_CD = [float((1.0 - 2.0 ** (-5 - h)) ** 256) for h in range(4)]

class _Cut(Exception):
    pass

def build(stage="full", cut=0):
    nc = bass.Bass("TRN2", target_bir_lowering=False)
    def din(name, shape, dt=F32):
        return nc.dram_tensor(name, shape, dt, kind="ExternalInput").ap()
    xall = din("xall", [NALL, D]); pos = din("pos", [1, NALL], I32)
    w_in = din("w_in", [D, 12800]); w_out = din("w_out", [D, D])
    w_gate = din("w_gate", [D, DFF]); w_up = din("w_up", [D, DFF]); w_down = din("w_down", [DFF, D])
    vecs = {n: din(n, [1, D]) for n in ("gn_g", "gn_b", "ln1_g", "ln1_b", "ln2_g", "ln2_b")}
    sinks = din("sinks", [1, 32]); cf_d = din("cf", [128, NCF]); cb_d = din("cb", [128, NCB], BF16)
    out_d = nc.dram_tensor("out", [T, D], F32, kind="ExternalOutput").ap()
    dbg_d = nc.dram_tensor("dbg", [T, D], F32, kind="ExternalOutput").ap() if stage != "full" else None

    P = Prog()
    names = [0]
    def sbt(shape, dt, off):
        names[0] += 1
        nbytes = int(np.prod(shape[1:])) * (2 if dt == BF16 else 4)
        assert off % 32 == 0 and off >= SB_BASE and off + nbytes <= SB_LIMIT, (shape, off, nbytes)
        return nc.alloc_sbuf_tensor_at(f"t{names[0]}", list(shape), dt, offset=off)
    class Lay:
        def __init__(self, start): self.o = start
        def take(self, shape, dt, name=None, track=True):
            nbytes = int(np.prod(shape[1:])) * (2 if dt == BF16 else 4)
            nbytes = (nbytes + 31) // 32 * 32
            t = sbt(shape, dt, self.o)
            b = P.region(name or f"r{self.o}", self.o, self.o + nbytes) if track else None
            self.o += nbytes
            return t, b

    wv_in = w_in.rearrange("(c p) n -> p c n", p=128)
    def psum_alloc(es):
        banks = []
        for i in range(8):
            t = es.enter_context(nc.psum_tensor(f"ps{i}", [128, 512], F32))
            banks.append((t, Buf(f"ps{i}")))
        return banks

    with contextlib.ExitStack() as es:
        banks = psum_alloc(es)
        bctr = [0]
        def bank():
            b = banks[bctr[0] % 8]; bctr[0] += 1; return b

        L0 = Lay(SB_BASE)
        cf, cfB = L0.take([128, NCF], F32, "cf")
        cb, cbB = L0.take([128, NCB], BF16, "cb")
        es_all, esB = L0.take([128, 32], F32, "es")
        P.add("sp", lambda e: e.dma_start(out=cf[:], in_=cf_d), writes=[cfB], dma=True)
        P.add("sp", lambda e: e.dma_start(out=cb[:], in_=cb_d), writes=[cbB], dma=True)
        P.add("sp", lambda e: e.dma_start(out=es_all[:], in_=sinks.partition_broadcast(128)), writes=[esB], dma=True)
        P.add("act", lambda e: e.activation(out=es_all[:], in_=es_all[:], func=AF.Exp), reads=[esB], writes=[esB])
        ident = cb[:, CB_ID:CB_ID + 128]; Rm = cb[:, CB_RM:CB_RM + 128]; ones = cb[:, CB_ONES:CB_ONES + 128]; SWm = cb[:, CB_SW:CB_SW + 128]
        eps_col = cf[:, CF_MISC:CF_MISC + 1]
        PERS_END = L0.o
        SIN_OFF = PERS_END

        def mm(out, lhsT, rhs, start, stop, reads, wb):
            P.add("pe", lambda e: e.matmul(out, lhsT=lhsT, rhs=rhs, start=start, stop=stop), reads=reads, writes=[wb])
        def tr(out, in_, reads, wb):
            P.add("pe", lambda e: e.transpose(out=out, in_=in_, identity=ident), reads=list(reads) + [cbB], writes=[wb])
        def act(out, in_, func, reads, writes, **kw):
            P.add("act", lambda e: e.activation(out=out, in_=in_, func=func, **kw), reads=reads, writes=writes)
        def tt(eng, out, in0, in1, op, reads, writes):
            P.add(eng, lambda e: e.tensor_tensor(out=out, in0=in0, in1=in1, op=op), reads=reads, writes=writes)
        def ts(eng, out, in0, s1, s2, op0, op1, reads, writes):
            if op1 is None:
                P.add(eng, lambda e: e.tensor_scalar(out=out, in0=in0, scalar1=s1, scalar2=None, op0=op0), reads=reads, writes=writes)
            else:
                P.add(eng, lambda e: e.tensor_scalar(out=out, in0=in0, scalar1=s1, scalar2=s2, op0=op0, op1=op1), reads=reads, writes=writes)
        def stt(eng, out, in0, scalar, in1, op0, op1, reads, writes):
            P.add(eng, lambda e: e.scalar_tensor_tensor(out=out, in0=in0, scalar=scalar, in1=in1, op0=op0, op1=op1), reads=reads, writes=writes)
        def cpy(eng, out, in_, reads, writes):
            P.add(eng, lambda e: e.tensor_copy(out=out, in_=in_), reads=reads, writes=writes)
        def wdma(out, in_, wb):
            P.add("pool", lambda e: e.dma_start(out=out, in_=in_), writes=[wb], dma=True)

        def gen_sincos(n, posf, posfB, invcol, tmp, cos_o, sin_o, cosB, sinB):
            (ang, angB), (y, yB), (ki, kiB), (r, rB), (m, mB) = tmp
            ts("dve", ang[:, :n], posf, invcol, None, ALU.mult, None, [posfB, cfB], [angB])
            for shift, o, oB in ((0.0, sin_o, sinB), (PI / 2, cos_o, cosB)):
                ts("dve", y[:, :n], ang[:, :n], 1.0 / TWO_PI, shift / TWO_PI, ALU.mult, ALU.add, [angB], [yB])
                cpy("dve", ki[:, :n], y[:, :n], [yB], [kiB])
                cpy("dve", y[:, :n], ki[:, :n], [kiB], [yB])
                if shift:
                    ts("dve", r[:, :n], ang[:, :n], shift, None, ALU.add, None, [angB], [rB])
                    stt("dve", r[:, :n], y[:, :n], -C1, r[:, :n], ALU.mult, ALU.add, [yB, rB], [rB])
                else:
                    stt("dve", r[:, :n], y[:, :n], -C1, ang[:, :n], ALU.mult, ALU.add, [yB, angB], [rB])
                stt("dve", r[:, :n], y[:, :n], -C2, r[:, :n], ALU.mult, ALU.add, [yB, rB], [rB])
                ts("dve", m[:, :n], r[:, :n], PI, -TWO_PI, ALU.is_gt, ALU.mult, [rB], [mB])
                tt("dve", r[:, :n], r[:, :n], m[:, :n], ALU.add, [rB, mB], [rB])
                ts("dve", m[:, :n], r[:, :n], -PI, TWO_PI, ALU.is_lt, ALU.mult, [rB], [mB])
                tt("dve", r[:, :n], r[:, :n], m[:, :n], ALU.add, [rB, mB], [rB])
                act(o, r[:, :n], AF.Sin, [rB], [oB])

        def ln_rows(src, srcB, dst, dstB, gv, gB, bv, bB, st, stB, mv, mvB, width=2048):
            nchk = width // 512
            for k in range(nchk):
                P.add("dve", lambda e, k=k: e.bn_stats(out=st[:, k * 6:(k + 1) * 6], in_=src[:, k * 512:(k + 1) * 512]), reads=[srcB], writes=[stB])
            P.add("dve", lambda e: e.bn_aggr(out=mv[:, 0:2], in_=st[:, 0:nchk * 6]), reads=[stB], writes=[mvB])
            act(mv[:, 2:3], mv[:, 1:2], AF.Sqrt, [mvB, cfB], [mvB], bias=eps_col, scale=1.0)
            P.add("dve", lambda e: e.reciprocal(out=mv[:, 2:3], in_=mv[:, 2:3]), reads=[mvB], writes=[mvB])
            ts("dve", dst, src, mv[:, 0:1], mv[:, 2:3], ALU.subtract, ALU.mult, [srcB, mvB], [dstB])
            tt("dve", dst, dst, gv, ALU.mult, [dstB, gB], [dstB])
            tt("dve", dst, dst, bv, ALU.add, [dstB, bB], [dstB])

        LA = Lay(SIN_OFF)
        SIN, SINB = LA.take([128, 8, 512], F32, "SIN")
        WK = []; WV = []
        for i in range(4): WK.append(LA.take([128, 16, 256], BF16, f"WK{i}"))
        for i in range(8): WV.append(LA.take([128, 16, 256], BF16, f"WV{i}"))
        xb = [LA.take([128, 2, 2048], BF16, f"xb{i}") for i in range(2)]
        xTc = [LA.take([128, 16, 256], BF16, f"xTc{i}") for i in range(2)]
        posi = LA.take([128, 256], I32, "posi"); posf = LA.take([128, 256], F32, "posf")
        tmpA = [LA.take([128, 256], F32 if k != 2 else I32, f"tmpA{k}") for k in range(5)]
        csA = [(LA.take([128, 256], F32, f"cosA{i}"), LA.take([128, 256], F32, f"sinA{i}")) for i in range(2)]
        tuA = [(LA.take([128, 2, 256], F32, f"tA{i}"), LA.take([128, 2, 256], F32, f"uA{i}")) for i in range(2)]
        krA = [LA.take([128, 2, 256], BF16, f"krA{i}") for i in range(2)]
        ktm, ktmB = LA.take([128, 2, 1024], BF16, "ktmA")
        vtm, vtmB = LA.take([128, 2, 2048], BF16, "vtmA")

        P.add("pool", lambda e: e.memset(SIN[:], 0.0), writes=[SINB])
        for i in range(4):
            wdma(WK[i][0][:], wv_in[:, :, 1024 + i * 256: 1024 + (i + 1) * 256], WK[i][1])
        xall_t = xall.rearrange("(c t p) d -> c p t d", p=128, t=2)
        pos_c = pos.rearrange("o (c n) -> c o n", n=256)
        def load_chunk(c):
            s = c % 2
            wdma(xb[s][0][:], xall_t[c], xb[s][1])
        load_chunk(0)
        for i in range(8):
            wdma(WV[i][0][:], wv_in[:, :, 2048 + i * 256: 2048 + (i + 1) * 256], WV[i][1])
        NCHA = NPRI // CH
        for c in range(NCHA):
            s = c % 2
            if c + 1 < NCHA: load_chunk(c + 1)
            xbt, xbB = xb[s]; xT, xTB = xTc[s]
            P.add("sp", lambda e, c=c: e.dma_start(out=posi[0][:], in_=pos_c[c].partition_broadcast(128)), writes=[posi[1]], dma=True)
            cpy("dve", posf[0][:], posi[0][:], [posi[1]], [posf[1]])
            (cosT, cosB), (sinT, sinB) = csA[s]
            gen_sincos(256, posf[0][:], posf[1], cf[:, CF_INVR:CF_INVR + 1], tmpA, cosT[:], sinT[:], cosB, sinB)
            for g in range(4):
                pt, pB = bank(); ptb = pt[:].bitcast(BF16)
                for dcl in range(4):
                    for t2 in range(2):
                        dc = g * 4 + dcl
                        tr(ptb[:, dcl * 256 + t2 * 128: dcl * 256 + (t2 + 1) * 128], xbt[:, t2, dc * 128:(dc + 1) * 128], [xbB], pB)
                act(xT[:, g * 4:(g + 1) * 4, :], ptb.rearrange("p (a n) -> p a n", a=4), AF.Copy, [pB], [xTB])
            for h in range(4):
                pt, pB = bank(); p3 = pt[:].rearrange("p (a n) -> p a n", a=2)
                for half in range(2):
                    col = h * 256 + half * 128
                    wt, wB = WK[col // 256]
                    for dc in range(16):
                        mm(p3[:, half, :], wt[:, dc, col % 256: col % 256 + 128], xT[:, dc, :], dc == 0, dc == 15, [wB, xTB], pB)
                (tT, tB), (uT, uB) = tuA[h % 2]
                krT, krB = krA[h % 2]
                cb3 = cosT[:].unsqueeze(1).to_broadcast([128, 2, 256]); sb3 = sinT[:].unsqueeze(1).to_broadcast([128, 2, 256])
                tt("dve", tT[:], p3, cb3, ALU.mult, [pB, cosB], [tB])
                tt("dve", uT[:], p3, sb3, ALU.mult, [pB, sinB], [uB])
                tt("dve", krT[:, 0, :], tT[:, 0, :], uT[:, 1, :], ALU.subtract, [tB, uB], [krB])
                tt("dve", krT[:, 1, :], tT[:, 1, :], uT[:, 0, :], ALU.add, [tB, uB], [krB])
                pt2, pB2 = bank(); p2b = pt2[:].bitcast(BF16)
                for t2 in range(2):
                    for half in range(2):
                        tr(p2b[:, (t2 * 2 + half) * 128:(t2 * 2 + half + 1) * 128], krT[:, half, t2 * 128:(t2 + 1) * 128], [krB], pB2)
                for t2 in range(2):
                    wcol = cf[:, CF_WPRI + h * 24 + c * 2 + t2: CF_WPRI + h * 24 + c * 2 + t2 + 1]
                    act(ktm[:, t2, h * 256:(h + 1) * 256], p2b[:, t2 * 256:(t2 + 1) * 256], AF.Identity, [pB2, cfB], [ktmB], scale=wcol)
            for t2 in range(2):
                for cbk in range(4):
                    pt, pB = bank()
                    for hf in range(2):
                        wt, wB = WV[cbk * 2 + hf]
                        for dc in range(16):
                            mm(pt[:, hf * 256:(hf + 1) * 256], xT[:, dc, t2 * 128:(t2 + 1) * 128], wt[:, dc, :], dc == 0, dc == 15, [wB, xTB], pB)
                    act(vtm[:, t2, cbk * 512:(cbk + 1) * 512], pt[:], AF.Copy, [pB], [vtmB])
            for h in range(4):
                for dt_ in range(2):
                    pt, pB = bank()
                    for t2 in range(2):
                        mm(pt[:], ktm[:, t2, h * 256 + dt_ * 128: h * 256 + (dt_ + 1) * 128], vtm[:, t2, h * 512:(h + 1) * 512],
                           t2 == 0, t2 == 1, [ktmB, vtmB], pB)
                    tt("dve", SIN[:, h * 2 + dt_, :], SIN[:, h * 2 + dt_, :], pt[:], ALU.add, [SINB, pB], [SINB])

        if stage == "sin":
            obs = []
            for idx in range(8):
                ob = Buf(f"so{idx}"); obs.append(ob)
                P.add("sp", lambda e, idx=idx: e.dma_start(out=dbg_d[idx * 128:(idx + 1) * 128, 0:512], in_=SIN[:, idx, :]), reads=[SINB], writes=[ob], dma=True)
            P.add("sp", lambda e: None, reads=obs)
            sem_list = []
            def sems():
                s_ = es.enter_context(nc.semaphore(f"s{len(sem_list)}")); sem_list.append(s_); return s_
            with nc.Block() as block:
                P.emit(block, sems)
            return nc

        LB = Lay(SIN_OFF + 16384)
        xTa, xTaB = LB.take([128, 16, 1152], BF16, "xTa")
        NWS = 3
        wst = [LB.take([128, 16, 256], BF16, f"wst{i}") for i in range(NWS)]
        retg, retgB = LB.take([128, 8, 2048], BF16, "retg")
        B_COMMON = LB.o
        L1 = Lay(B_COMMON)
        qT, qTB = L1.take([128, 2, 1024], BF16, "qT"); qTs, qTsB = L1.take([128, 2, 1024], BF16, "qTs")
        kT, kTB = L1.take([128, 2, 1024], BF16, "kT"); ktmo, ktmoB = L1.take([128, 8, 256], BF16, "ktmo")
        vtmo, vtmoB = L1.take([128, 8, 512], BF16, "vtmo")
        cosr, cosrB = L1.take([128, 1024], F32, "cosr"); sinr, sinrB = L1.take([128, 1024], F32, "sinr")
        posi1 = L1.take([128, 256], I32, "posi1"); posf1 = L1.take([128, 256], F32, "posf1")
        tmpB = [L1.take([128, 256], F32 if k != 2 else I32, f"tmpB{k}") for k in range(5)]
        rt = [L1.take([128, 512], F32, f"rt{k}") for k in range(4)]
        PTt = [L1.take([128, 2, 256], BF16, f"PT{k}") for k in range(2)]
        stbf, stbfB = L1.take([128, 2, 512], BF16, "stbf")
        gvec, gvB = L1.take([128, 2048], F32, "gvec"); bvec, bvB = L1.take([128, 2048], F32, "bvec")
        ytmp = [L1.take([128, 512], F32, f"ytmp{k}") for k in range(2)]
        stt_t, sttB = L1.take([128, 24], F32, "bnst"); mv_t, mvB = L1.take([128, 4], F32, "mv")
        xb1 = [L1.take([128, 2048], BF16, f"xb1{i}") for i in range(2)]

        xall_t1 = xall.rearrange("(c p) d -> c p d", p=128)
        for t9 in range(9):
            xbt, xbB = xb1[t9 % 2]
            wdma(xbt[:], xall_t1[23 + t9], xbB)
            for g in range(2):
                pt, pB = bank(); ptb = pt[:].bitcast(BF16)
                for dcl in range(8):
                    dc = g * 8 + dcl
                    tr(ptb[:, dcl * 128:(dcl + 1) * 128], xbt[:, dc * 128:(dc + 1) * 128], [xbB], pB)
                act(xTa[:, g * 8:(g + 1) * 8, t9 * 128:(t9 + 1) * 128], ptb.rearrange("p (a n) -> p a n", a=8), AF.Copy, [pB], [xTaB])
        for c4 in range(4):
            P.add("sp", lambda e, c4=c4: e.dma_start(out=posi1[0][:], in_=pos_c[12 + c4].partition_broadcast(128)), writes=[posi1[1]], dma=True)
            cpy("dve", posf1[0][:], posi1[0][:], [posi1[1]], [posf1[1]])
            gen_sincos(256, posf1[0][:], posf1[1], cf[:, CF_INVR:CF_INVR + 1], tmpB,
                       cosr[:, c4 * 256:(c4 + 1) * 256], sinr[:, c4 * 256:(c4 + 1) * 256], cosrB, sinrB)
        P.add("sp", lambda e: e.dma_start(out=gvec[:], in_=vecs["gn_g"].partition_broadcast(128)), writes=[gvB], dma=True)
        P.add("sp", lambda e: e.dma_start(out=bvec[:], in_=vecs["gn_b"].partition_broadcast(128)), writes=[bvB], dma=True)

        if stage == "tab":
            obs = [Buf("tb0"), Buf("tb1")]
            P.add("sp", lambda e: e.dma_start(out=dbg_d[0:128, 0:1024], in_=cosr[:]), reads=[cosrB], writes=[obs[0]], dma=True)
            P.add("sp", lambda e: e.dma_start(out=dbg_d[0:128, 1024:2048], in_=sinr[:]), reads=[sinrB], writes=[obs[1]], dma=True)
            P.add("sp", lambda e: None, reads=obs)
            sem_list = []
            def sems():
                s_ = es.enter_context(nc.semaphore(f"s{len(sem_list)}")); sem_list.append(s_); return s_
            with nc.Block() as block:
                P.emit(block, sems)
            return nc

        wctr = [0]
        def wload(col0):
            wt, wB = wst[wctr[0] % NWS]; wctr[0] += 1
            wdma(wt[:], wv_in[:, :, col0:col0 + 256], wB)
            return wt, wB

        OWN = 128
        def proj_fm_rot(wt, wB, out_bf, outB, out_s, outsB, h):
            for tb in range(2):
                pa, paB = bank(); pb_, pbB = bank()
                for half, (pp, ppB) in enumerate(((pa, paB), (pb_, pbB))):
                    for dc in range(16):
                        mm(pp[:], wt[:, dc, half * 128:(half + 1) * 128], xTa[:, dc, OWN + tb * 512: OWN + (tb + 1) * 512], dc == 0, dc == 15, [wB, xTaB], ppB)
                cs = cosr[:, tb * 512:(tb + 1) * 512]; sn = sinr[:, tb * 512:(tb + 1) * 512]
                (r0, r0B), (r1, r1B), (r2, r2B), (r3, r3B) = rt
                tt("dve", r0[:], pa[:], cs, ALU.mult, [paB, cosrB], [r0B])
                tt("dve", r1[:], pb_[:], sn, ALU.mult, [pbB, sinrB], [r1B])
                tt("dve", r2[:], pb_[:], cs, ALU.mult, [pbB, cosrB], [r2B])
                tt("dve", r3[:], pa[:], sn, ALU.mult, [paB, sinrB], [r3B])
                if out_s is None:
                    tt("dve", out_bf[:, 0, tb * 512:(tb + 1) * 512], r0[:], r1[:], ALU.subtract, [r0B, r1B], [outB])
                    tt("dve", out_bf[:, 1, tb * 512:(tb + 1) * 512], r2[:], r3[:], ALU.add, [r2B, r3B], [outB])
                else:
                    tt("dve", r0[:], r0[:], r1[:], ALU.subtract, [r0B, r1B], [r0B])
                    tt("dve", r2[:], r2[:], r3[:], ALU.add, [r2B, r3B], [r2B])
                    qd = cf[:, CF_QDEC + h * 256: CF_QDEC + (h + 1) * 256].unsqueeze(1).to_broadcast([128, 2, 256])
                    for half, (rr, rrB) in enumerate(((r0, r0B), (r2, r2B))):
                        act(out_bf[:, half, tb * 512:(tb + 1) * 512], rr[:], AF.Copy, [rrB], [outB])
                        tt("dve", out_s[:, half, tb * 512:(tb + 1) * 512].rearrange("p (a n) -> p a n", a=2),
                           rr[:].rearrange("p (a n) -> p a n", a=2), qd, ALU.mult, [rrB, cfB], [outsB])

        for h in range(4):
            wq = wload(h * 256); wk = wload(1024 + h * 256)
            proj_fm_rot(wq[0], wq[1], qT, qTB, qTs, qTsB, h)
            wv0 = wload(2048 + h * 512)
            proj_fm_rot(wk[0], wk[1], kT, kTB, None, None, h)
            wv1 = wload(2048 + h * 512 + 256)
            for t8 in range(8):
                pt, pB = bank(); ptb = pt[:].bitcast(BF16)
                for half in range(2):
                    tr(ptb[:, half * 128:(half + 1) * 128], kT[:, half, t8 * 128:(t8 + 1) * 128], [kTB], pB)
                kcol = cf[:, CF_KDEC + h * 2 + t8 % 2: CF_KDEC + h * 2 + t8 % 2 + 1]
                act(ktmo[:, t8, :], ptb[:, 0:256], AF.Identity, [pB, cfB], [ktmoB], scale=kcol)
            for t8 in range(8):
                pt, pB = bank()
                for hf, (wt, wB) in enumerate((wv0, wv1)):
                    for dc in range(16):
                        mm(pt[:, hf * 256:(hf + 1) * 256], xTa[:, dc, OWN + t8 * 128: OWN + (t8 + 1) * 128], wt[:, dc, :], dc == 0, dc == 15, [wB, xTaB], pB)
                act(vtmo[:, t8, :], pt[:], AF.Copy, [pB], [vtmoB])
            for i in range(4):
                for dt_ in range(2):
                    act(stbf[:, dt_, :], SIN[:, h * 2 + dt_, :], AF.Copy, [SINB], [stbfB])
                PT, PTB = PTt[i % 2]
                for mt in range(2):
                    c0 = mt * 128
                    pt, pB = bank()
                    for dh in range(2):
                        mm(pt[:, c0:256], kT[:, dh, i * 256 + mt * 128: i * 256 + (mt + 1) * 128], qT[:, dh, i * 256 + c0: i * 256 + 256],
                           dh == 0, dh == 1, [kTB, qTB], pB)
                    dm = cf[:, CF_DM + (h * 2 + mt) * 256 + c0: CF_DM + (h * 2 + mt + 1) * 256]
                    tt("dve", PT[:, mt, c0:256], pt[:, c0:256], dm, ALU.mult, [pB, cfB], [PTB])
                for ct in range(2):
                    pt, pB = bank()
                    seq = [(PT[:, 0, ct * 128:(ct + 1) * 128], vtmo[:, i * 2, :], [PTB, vtmoB])]
                    if ct == 1:
                        seq.append((PT[:, 1, 128:256], vtmo[:, i * 2 + 1, :], [PTB, vtmoB]))
                    for dt_ in range(2):
                        seq.append((qTs[:, dt_, i * 256 + ct * 128: i * 256 + (ct + 1) * 128], stbf[:, dt_, :], [qTsB, stbfB]))
                    for k, (l_, r_, rd) in enumerate(seq):
                        mm(pt[:], l_, r_, k == 0, k == len(seq) - 1, rd, pB)
                    yt, ytB = ytmp[ct]
                    P.add("dve", lambda e, pt=pt: e.bn_stats(out=stt_t[:, 0:6], in_=pt[:]), reads=[pB], writes=[sttB])
                    P.add("dve", lambda e: e.bn_aggr(out=mv_t[:, 0:2], in_=stt_t[:, 0:6]), reads=[sttB], writes=[mvB])
                    act(mv_t[:, 2:3], mv_t[:, 1:2], AF.Sqrt, [mvB, cfB], [mvB], bias=eps_col, scale=1.0)
                    P.add("dve", lambda e: e.reciprocal(out=mv_t[:, 2:3], in_=mv_t[:, 2:3]), reads=[mvB], writes=[mvB])
                    ts("dve", yt[:], pt[:], mv_t[:, 0:1], mv_t[:, 2:3], ALU.subtract, ALU.mult, [pB, mvB], [ytB])
                    tt("dve", yt[:], yt[:], gvec[:, h * 512:(h + 1) * 512], ALU.mult, [ytB, gvB], [ytB])
                    tt("dve", retg[:, i * 2 + ct, h * 512:(h + 1) * 512], yt[:], bvec[:, h * 512:(h + 1) * 512], ALU.add, [ytB, bvB], [retgB])
                if i < 3:
                    for dt_ in range(2):
                        pt, pB = bank()
                        for t2 in range(2):
                            mm(pt[:], ktmo[:, i * 2 + t2, dt_ * 128:(dt_ + 1) * 128], vtmo[:, i * 2 + t2, :], t2 == 0, t2 == 1, [ktmoB, vtmoB], pB)
                        stt("dve", SIN[:, h * 2 + dt_, :], SIN[:, h * 2 + dt_, :], GAM_CD[h], pt[:], ALU.mult, ALU.add, [SINB, pB], [SINB])

        def dump_tokmajor_bf(src, srcB, lay_off):
            LD = Lay(lay_off)
            dts = [LD.take([128, 2048], F32, f"dbgt{k}") for k in range(2)]
            obs = []
            for t8 in range(8):
                dt_, dB = dts[t8 % 2]
                cpy("dve", dt_[:], src[:, t8, :], [srcB], [dB])
                ob = Buf(f"dbgo{t8}"); obs.append(ob)
                P.add("sp", lambda e, t8=t8, dt_=dt_: e.dma_start(out=dbg_d[t8 * 128:(t8 + 1) * 128, :], in_=dt_[:]), reads=[dB], writes=[ob], dma=True)
            P.add("sp", lambda e: None, reads=obs)

        def finish():
            sem_list = []
            def sems():
                s = es.enter_context(nc.semaphore(f"s{len(sem_list)}")); sem_list.append(s); return s
            with nc.Block() as block:
                P.emit(block, sems)
            return nc

        def zero_out_and_finish(lay_off):
            LZ = Lay(lay_off)
            zt, zB = LZ.take([128, 2048], F32, "zt")
            P.add("pool", lambda e: e.memset(zt[:], 0.0), writes=[zB])
            obs = []
            for t8 in range(8):
                ob = Buf(f"zo{t8}"); obs.append(ob)
                P.add("sp", lambda e, t8=t8: e.dma_start(out=out_d[t8 * 128:(t8 + 1) * 128, :], in_=zt[:]), reads=[zB], writes=[ob], dma=True)
            P.add("sp", lambda e: None, reads=obs)
            return finish()
        if stage == "ret":
            dump_tokmajor_bf(retg, retgB, SIN_OFF)
            return zero_out_and_finish(SIN_OFF + 16384)

        L2 = Lay(B_COMMON)
        attT, attTB = L2.take([128, 16, 1024], BF16, "attT")
        akT, akTB = L2.take([128, 4, 1152], BF16, "akT")
        avd, avdB = L2.take([128, 9, 512], BF16, "avd")
        cosa, cosaB = L2.take([128, 1152], F32, "cosa"); sina, sinaB = L2.take([128, 1152], F32, "sina")
        L2T = Lay(L2.o)
        posi2 = L2T.take([128, 384], I32, "posi2"); posf2 = L2T.take([128, 384], F32, "posf2")
        tmpC = [L2T.take([128, 384], F32 if k != 2 else I32, f"tmpC{k}") for k in range(5)]
        try:
            for c3 in range(3):
                P.add("sp", lambda e, c3=c3: e.dma_start(out=posi2[0][:], in_=pos[:, NPRI - 128 + c3 * 384: NPRI - 128 + (c3 + 1) * 384].partition_broadcast(128)),
                      writes=[posi2[1]], dma=True)
                cpy("dve", posf2[0][:], posi2[0][:], [posi2[1]], [posf2[1]])
                gen_sincos(384, posf2[0][:], posf2[1], cf[:, CF_INVA:CF_INVA + 1], tmpC,
                           cosa[:, c3 * 384:(c3 + 1) * 384], sina[:, c3 * 384:(c3 + 1) * 384], cosaB, sinaB)

            aqT = [L2.take([128, 4, 1024], BF16, "aqT0")] * 2
            qb_bf = [L2.take([128, 512], BF16, f"qbbf{k}") for k in range(2)]
            ra = [L2.take([128, 512], F32, "ra0")] * 2
            rb = [L2.take([128, 512], F32, "rb0")] * 2
            PTa = [L2.take([128, 4, 512], BF16, f"PTa{k}") for k in range(2)]
            den = [L2.take([128, 512], F32, "den0")] * 2

            if cut == 1: raise _Cut()
            def att_rot(pp, ppB, n, tok0, out_ap, outB, k):
                qbt, qbB = qb_bf[k % 2]; (a_, aB) = ra[k % 2]; (b_, bB) = rb[k % 2]
                cpy("dve", qbt[:, :n], pp[:, :n], [ppB], [qbB])
                tt("dve", a_[:, :n], pp[:, :n], cosa[:, tok0:tok0 + n], ALU.mult, [ppB, cosaB], [aB])
                p2, p2B = bank()
                mm(p2[:, :n], Rm, qbt[:, :n], True, True, [cbB, qbB], p2B)
                tt("dve", b_[:, :n], p2[:, :n], sina[:, tok0:tok0 + n], ALU.mult, [p2B, sinaB], [bB])
                tt("dve", out_ap, a_[:, :n], b_[:, :n], ALU.add, [aB, bB], [outB])

            AK0 = 8192; AV0 = 8448; AQ0 = 6144
            cnt = 0
            wk_, wkB = wload(AK0)
            for ch2 in range(2):
                for (t0_, n_) in ((0, 512), (512, 512), (1024, 128)):
                    pt, pB = bank()
                    for dc in range(16):
                        mm(pt[:, :n_], wk_[:, dc, ch2 * 128:(ch2 + 1) * 128], xTa[:, dc, t0_:t0_ + n_], dc == 0, dc == 15, [wkB, xTaB], pB)
                    att_rot(pt, pB, n_, t0_, akT[:, ch2, t0_:t0_ + n_], akTB, cnt); cnt += 1
                    p2, p2B = bank()
                    mm(p2[:, :n_], SWm, akT[:, ch2, t0_:t0_ + n_], True, True, [cbB, akTB], p2B)
                    act(akT[:, 2 + ch2, t0_:t0_ + n_], p2[:, :n_], AF.Copy, [p2B], [akTB])
            if cut == 2: raise _Cut()
            wv_, wvB = wload(AV0)
            for t9 in range(9):
                pt, pB = bank()
                for dc in range(16):
                    mm(pt[:, :256], xTa[:, dc, t9 * 128:(t9 + 1) * 128], wv_[:, dc, :], dc == 0, dc == 15, [wvB, xTaB], pB)
                for r_ in range(2):
                    act(avd[:, t9, :].rearrange("p (k r d) -> p k r d", k=4, r=2)[:, :, r_, :], pt[:, :256].rearrange("p (k d) -> p k d", k=4), AF.Copy, [pB], [avdB])
            if cut == 3: raise _Cut()
            for kvg in range(4):
                aq, aqB = aqT[kvg % 2]
                for blk in range(2):
                    wt, wB = wload(AQ0 + kvg * 512 + blk * 256)
                    for cl in range(2):
                        for tb in range(2):
                            pt, pB = bank()
                            for dc in range(16):
                                mm(pt[:], wt[:, dc, cl * 128:(cl + 1) * 128], xTa[:, dc, OWN + tb * 512: OWN + (tb + 1) * 512], dc == 0, dc == 15, [wB, xTaB], pB)
                            att_rot(pt, pB, 512, OWN + tb * 512, aq[:, blk * 2 + cl, tb * 512:(tb + 1) * 512], aqB, cnt); cnt += 1
                if cut == 4: raise _Cut()
                for qb in range(8):
                    PT, PTB = PTa[qb % 2]
                    for kt in range(2):
                        keyt = qb + kt
                        if kt == 1: mcol = CB_MCUR
                        else: mcol = CB_MPREV0 if qb == 0 else CB_MPREV
                        for par in range(2):
                            pt, pB = bank()
                            mm(pt[:], ident, cb[:, mcol:mcol + 512], True, False, [cbB], pB)
                            for a4 in range(4):
                                mm(pt[:, a4 * 128:(a4 + 1) * 128], akT[par * 64:(par + 1) * 64, (kvg // 2) + (0 if kvg % 2 == par else 2), keyt * 128:(keyt + 1) * 128],
                                   aq[par * 64:(par + 1) * 64, a4, qb * 128:(qb + 1) * 128], False, a4 == 3, [akTB, aqB], pB)
                            act(PT[:, kt * 2 + par, :], pt[:], AF.Exp, [pB], [PTB], scale=0.125)
                    if cut == 5: raise _Cut()
                    for par in range(2):
                        po, poB = bank(); pd, pdB = bank()
                        for kt in range(2):
                            mm(po[:], avd[:, qb + kt, kvg * 128:(kvg + 1) * 128], PT[:, kt * 2 + par, :], kt == 0, kt == 1, [avdB, PTB], poB)
                        for kt in range(2):
                            mm(pd[:], ones, PT[:, kt * 2 + par, :], kt == 0, kt == 1, [cbB, PTB], pdB)
                        if cut == 6: raise _Cut()
                        dn, dnB = den[par]
                        esb = es_all[:, kvg * 8 + par: kvg * 8 + 8: 2].unsqueeze(2).to_broadcast([128, 4, 128])
                        tt("dve", dn[:].rearrange("p (a n) -> p a n", a=4), pd[:].rearrange("p (a n) -> p a n", a=4), esb, ALU.add, [pdB, esB], [dnB])
                        P.add("dve", lambda e, dn=dn: e.reciprocal(out=dn[:], in_=dn[:]), reads=[dnB], writes=[dnB])
                        lo = par * 64
                        tt("dve", attT[lo:lo + 64, kvg * 4:(kvg + 1) * 4, qb * 128:(qb + 1) * 128],
                           po[lo:lo + 64, :].rearrange("p (a n) -> p a n", a=4), dn[lo:lo + 64, :].rearrange("p (a n) -> p a n", a=4),
                           ALU.mult, [poB, dnB], [attTB])


        except _Cut:
            dump_tokmajor_bf(retg, retgB, SIN_OFF)
            return zero_out_and_finish(SIN_OFF + 16384)
        def dump_featmajor(src, srcB, lay_off):
            LDm = Lay(lay_off)
            dts = [LDm.take([128, 1024], F32, f"dfm{k}") for k in range(2)]
            obs = []
            for ch in range(16):
                dt_, dB = dts[ch % 2]
                cpy("dve", dt_[:], src[:, ch, :], [srcB], [dB])
                ob = Buf(f"dfo{ch}"); obs.append(ob)
                P.add("sp", lambda e, ch=ch, dt_=dt_: e.dma_start(out=dbg_d[ch * 64:(ch + 1) * 64, :].rearrange("r (two n) -> (r two) n", two=2), in_=dt_[:]),
                      reads=[dB], writes=[ob], dma=True)
            P.add("sp", lambda e: None, reads=obs)
        if stage == "att":
            dump_featmajor(attT, attTB, SIN_OFF)
            return zero_out_and_finish(SIN_OFF + 8192)

        L3 = Lay(B_COMMON + 32768)
        g1 = [L3.take([128, 256], F32, f"g1{k}") for k in range(2)]
        g2 = [L3.take([128, 256], F32, f"g2{k}") for k in range(2)]
        g3 = [L3.take([128, 512], F32, f"g3{k}") for k in range(2)]
        RG0 = 4096; GA0 = 8704; GB0 = 10752
        for blk in range(8):
            wg = wload(RG0 + blk * 256); wa = wload(GA0 + blk * 256)
            for t8 in range(8):
                pg, pgB = bank(); pa, paB = bank()
                for (pp, ppB), (wt, wB) in (((pg, pgB), wg), ((pa, paB), wa)):
                    for dc in range(16):
                        mm(pp[:, :256], xTa[:, dc, OWN + t8 * 128: OWN + (t8 + 1) * 128], wt[:, dc, :], dc == 0, dc == 15, [wB, xTaB], ppB)
                (t1, t1B) = g1[t8 % 2]; (t2_, t2B) = g2[t8 % 2]
                act(t1[:], pg[:, :256], AF.Silu, [pgB], [t1B])
                act(t2_[:], pa[:, :256], AF.Sigmoid, [paB], [t2B])
                tt("dve", t1[:], t1[:], t2_[:], ALU.mult, [t1B, t2B], [t1B])
                tt("dve", retg[:, t8, blk * 256:(blk + 1) * 256], retg[:, t8, blk * 256:(blk + 1) * 256], t1[:], ALU.mult, [retgB, t1B], [retgB])
        for blk in range(8):
            wb_ = wload(GB0 + blk * 256)
            for cl in range(2):
                for tb in range(2):
                    pt, pB = bank()
                    for dc in range(16):
                        mm(pt[:], wb_[0][:, dc, cl * 128:(cl + 1) * 128], xTa[:, dc, OWN + tb * 512: OWN + (tb + 1) * 512], dc == 0, dc == 15, [wb_[1], xTaB], pB)
                    (t3, t3B) = g3[(cl * 2 + tb) % 2]
                    act(t3[:], pt[:], AF.Sigmoid, [pB], [t3B])
                    ch_ = blk * 2 + cl
                    tt("dve", attT[:, ch_, tb * 512:(tb + 1) * 512], attT[:, ch_, tb * 512:(tb + 1) * 512], t3[:], ALU.mult, [attTB, t3B], [attTB])
        for t8 in range(8):
            for g in range(2):
                pt, pB = bank(); ptb = pt[:].bitcast(BF16)
                for dcl in range(8):
                    dc = g * 8 + dcl
                    tr(ptb[:, dcl * 128:(dcl + 1) * 128], retg[:, t8, dc * 128:(dc + 1) * 128], [retgB], pB)
                tt("dve", attT[:, g * 8:(g + 1) * 8, t8 * 128:(t8 + 1) * 128], attT[:, g * 8:(g + 1) * 8, t8 * 128:(t8 + 1) * 128],
                   ptb.rearrange("p (a n) -> p a n", a=8), ALU.add, [attTB, pB], [attTB])
        mT, mTB = attT, attTB
        if stage == "merged":
            dump_featmajor(attT, attTB, SIN_OFF)
            return zero_out_and_finish(SIN_OFF + 8192)

        LC = Lay(SIN_OFF)
        acc = [LC.take([128, 2048], F32, f"acc{t8}") for t8 in range(8)]
        assert LC.o <= B_COMMON, (LC.o, B_COMMON)
        CFREE = LC.o
        LCa = Lay(CFREE)
        gv1, gv1B = LCa.take([128, 2048], F32, "gv1"); bv1, bv1B = LCa.take([128, 2048], F32, "bv1")
        st1, st1B = LCa.take([128, 24], F32, "st1"); mv1, mv1B = LCa.take([128, 4], F32, "mv1")
        wso = [LCa.take([128, 16, 256], BF16, f"wso{i}") for i in range(3)]
        assert LCa.o <= B_COMMON, (LCa.o, B_COMMON)
        LCb = Lay(B_COMMON + 32768)
        h1T, h1TB = LCb.take([128, 16, 1024], BF16, "h1T")
        H1T_END = LCb.o
        h1b = [LCb.take([128, 2048], BF16, f"h1b{k}") for k in range(2)]
        for t8 in range(8):
            P.add("sp", lambda e, t8=t8: e.dma_start(out=acc[t8][0][:], in_=xall[NPRI + t8 * 128: NPRI + (t8 + 1) * 128, :]), writes=[acc[t8][1]], dma=True)
        P.add("sp", lambda e: e.dma_start(out=gv1[:], in_=vecs["ln1_g"].partition_broadcast(128)), writes=[gv1B], dma=True)
        P.add("sp", lambda e: e.dma_start(out=bv1[:], in_=vecs["ln1_b"].partition_broadcast(128)), writes=[bv1B], dma=True)
        wv_out = w_out.rearrange("(c p) n -> p c n", p=128)
        woc = [0]
        def wload_o(blk):
            wt, wB = wso[woc[0] % 3]; woc[0] += 1
            wdma(wt[:], wv_out[:, :, blk * 256:(blk + 1) * 256], wB)
            return wt, wB
        pend = [wload_o(0), wload_o(1)]
        for blk in range(8):
            wt, wB = pend.pop(0)
            if blk + 2 < 8: pend.append(wload_o(blk + 2))
            for t8 in range(8):
                pt, pB = bank()
                for dc in range(16):
                    mm(pt[:, :256], mT[:, dc, t8 * 128:(t8 + 1) * 128], wt[:, dc, :], dc == 0, dc == 15, [wB, mTB], pB)
                a_, aB = acc[t8]
                stt("dve", a_[:, blk * 256:(blk + 1) * 256], a_[:, blk * 256:(blk + 1) * 256], ALPHA, pt[:, :256], ALU.mult, ALU.add, [aB, pB], [aB])
        for t8 in range(8):
            a_, aB = acc[t8]
            ln_rows(a_[:], aB, a_[:], aB, gv1[:], gv1B, bv1[:], bv1B, st1, st1B, mv1, mv1B)
            hb, hbB = h1b[t8 % 2]
            act(hb[:], a_[:], AF.Copy, [aB], [hbB])
            for g in range(2):
                pt, pB = bank(); ptb = pt[:].bitcast(BF16)
                for dcl in range(8):
                    dc = g * 8 + dcl
                    tr(ptb[:, dcl * 128:(dcl + 1) * 128], hb[:, dc * 128:(dc + 1) * 128], [hbB], pB)
                act(h1T[:, g * 8:(g + 1) * 8, t8 * 128:(t8 + 1) * 128], ptb.rearrange("p (a n) -> p a n", a=8), AF.Copy, [pB], [h1TB])
            if stage == "h1":
                P.add("sp", lambda e, t8=t8: e.dma_start(out=dbg_d[t8 * 128:(t8 + 1) * 128, :], in_=acc[t8][0][:]), reads=[aB], dma=True)
            act(a_[:], a_[:], AF.Identity, [aB], [aB], scale=ALPHA)

        LD_ = Lay(B_COMMON)
        wgu = [LD_.take([128, 16, 256], BF16, f"wgu{i}") for i in range(4)]
        assert LD_.o <= B_COMMON + 32768
        LDa = Lay(CFREE)
        wdn = [LDa.take([128, 4, 2048], BF16, f"wdn{k}") for k in range(2)]
        aTg = [LDa.take([128, 4, 1024], BF16, "aTg0")]
        sgt = [LDa.take([128, 512], F32, f"sgt{k}") for k in range(2)]
        assert LDa.o <= B_COMMON, (LDa.o, B_COMMON)
        LDc = Lay(H1T_END)
        aTg.append(LDc.take([128, 4, 1024], BF16, "aTg1"))
        wv_g = w_gate.rearrange("(c p) n -> p c n", p=128); wv_u = w_up.rearrange("(c p) n -> p c n", p=128)
        wv_d = w_down.rearrange("(g c p) n -> g p c n", p=128, c=4)
        NFG = DFF // 512
        guc = [0]
        def wload_gu(fg, blk):
            r = []
            for src in (wv_g, wv_u):
                wt, wB = wgu[guc[0] % 4]; guc[0] += 1
                wdma(wt[:], src[:, :, fg * 512 + blk * 256: fg * 512 + (blk + 1) * 256], wB)
                r.append((wt, wB))
            return r
        seqb = [(fg, blk) for fg in range(NFG) for blk in range(2)]
        pend = [wload_gu(*seqb[0])]
        for si, (fg, blk) in enumerate(seqb):
            if blk == 0:
                wd, wdB = wdn[fg % 2]
                wdma(wd[:], wv_d[fg], wdB)
            (wg_, wgB), (wu_, wuB) = pend.pop(0)
            if si + 1 < len(seqb): pend.append(wload_gu(*seqb[si + 1]))
            aT, aTB = aTg[fg % 2]
            for cl in range(2):
                fc = blk * 2 + cl
                for tb in range(2):
                    pg, pgB = bank(); pu, puB = bank()
                    for (pp, ppB), (wt, wB) in (((pg, pgB), (wg_, wgB)), ((pu, puB), (wu_, wuB))):
                        for dc in range(16):
                            mm(pp[:], wt[:, dc, cl * 128:(cl + 1) * 128], h1T[:, dc, tb * 512:(tb + 1) * 512], dc == 0, dc == 15, [wB, h1TB], ppB)
                    sg, sgB = sgt[(cl * 2 + tb) % 2]
                    act(sg[:], pg[:], AF.Silu, [pgB], [sgB])
                    tt("dve", aT[:, fc, tb * 512:(tb + 1) * 512], sg[:], pu[:], ALU.mult, [sgB, puB], [aTB])
            if blk == 1:
                wd, wdB = wdn[fg % 2]
                for t8 in range(8):
                    a_, aB = acc[t8]
                    for cbk in range(4):
                        pt, pB = bank()
                        for fc in range(4):
                            mm(pt[:], aT[:, fc, t8 * 128:(t8 + 1) * 128], wd[:, fc, cbk * 512:(cbk + 1) * 512], fc == 0, fc == 3, [aTB, wdB], pB)
                        tt("dve", a_[:, cbk * 512:(cbk + 1) * 512], a_[:, cbk * 512:(cbk + 1) * 512], pt[:], ALU.add, [aB, pB], [aB])

        LE = Lay(B_COMMON)
        gv2, gv2B = LE.take([128, 2048], F32, "gv2"); bv2, bv2B = LE.take([128, 2048], F32, "bv2")
        st2, st2B = LE.take([128, 24], F32, "st2"); mv2, mv2B = LE.take([128, 4], F32, "mv2")
        P.add("sp", lambda e: e.dma_start(out=gv2[:], in_=vecs["ln2_g"].partition_broadcast(128)), writes=[gv2B], dma=True)
        P.add("sp", lambda e: e.dma_start(out=bv2[:], in_=vecs["ln2_b"].partition_broadcast(128)), writes=[bv2B], dma=True)
        outBs = [Buf(f"outd{t8}") for t8 in range(8)]
        for t8 in range(8):
            a_, aB = acc[t8]
            ln_rows(a_[:], aB, a_[:], aB, gv2[:], gv2B, bv2[:], bv2B, st2, st2B, mv2, mv2B)
            P.add("sp", lambda e, t8=t8: e.dma_start(out=out_d[t8 * 128:(t8 + 1) * 128, :], in_=acc[t8][0][:]), reads=[aB], writes=[outBs[t8]], dma=True)
        P.add("sp", lambda e: None, reads=outBs)
        return finish()


def _core_inputs(inputs, c):
    b, j = divmod(c, 4)
    x = np.asarray(inputs["x"]); positions = np.asarray(inputs["positions"])
    start = 1024 * j
    xall = np.zeros((NALL, D), np.float32); posall = np.zeros((1, NALL), np.int32)
    npri = min(start, NPRI)
    if npri:
        xall[NPRI - npri:NPRI] = x[b, start - npri:start]
        posall[0, NPRI - npri:NPRI] = positions[b, start - npri:start]
    xall[NPRI:] = x[b, start:start + T]
    posall[0, NPRI:] = positions[b, start:start + T]
    cf, cb = _const_tables(j)
    m = {"xall": xall, "pos": posall, "cf": cf, "cb": cb,
         "w_in": np.ascontiguousarray(inputs["w_in"][0]), "w_out": np.ascontiguousarray(inputs["w_out"][0]),
         "w_gate": np.ascontiguousarray(inputs["w_gate"][0]), "w_up": np.ascontiguousarray(inputs["w_up"][0]),
         "w_down": np.ascontiguousarray(inputs["w_down"][0]),
         "sinks": np.ascontiguousarray(inputs["att_sinks"]).reshape(1, 32).astype(np.float32)}
    for n, k in (("gn_g", "ret_gn_g"), ("gn_b", "ret_gn_b"), ("ln1_g", "ln1_g"), ("ln1_b", "ln1_b"), ("ln2_g", "ln2_g"), ("ln2_b", "ln2_b")):
        m[n] = np.ascontiguousarray(inputs[k]).reshape(1, D).astype(np.float32)
    return m


def kernel(**inputs):
    nc = build("full")
    in_maps = [_core_inputs(inputs, c) for c in range(8)]
    res = run_bass_kernel_spmd(nc, in_maps, core_ids=list(range(8)))
    out = np.zeros((2, SEQ, D), np.float32)
    for c in range(8):
        b, j = divmod(c, 4)
        out[b, 1024 * j:1024 * (j + 1)] = res.results[c]["out"]
    return out
```
